# Optimizing a Trainium2 kernel written in Bass

```python
import math
import jax, jax.numpy as jnp
from jax import lax
import numpy as np

D_MODEL = 2048
BATCH = 1
SEQ = 8192
DEPTH = 2

N_MIXERS = 2
N_GDN_LAYERS = (DEPTH + 1) // 2
N_HGRN_LAYERS = DEPTH // 2

GDN_K_HEADS = 16
GDN_V_HEADS = 32
GDN_HEAD_K = 128
GDN_HEAD_V = 128
GDN_KEY_DIM = GDN_K_HEADS * GDN_HEAD_K
GDN_VAL_DIM = GDN_V_HEADS * GDN_HEAD_V
GDN_CONV = 4
GDN_CONV_DIM = 2 * GDN_KEY_DIM + GDN_VAL_DIM
GDN_IN_DIM = GDN_CONV_DIM + GDN_VAL_DIM + 2 * GDN_V_HEADS

HGRN_EXPAND = 128
HGRN_HEADS = D_MODEL // HGRN_EXPAND
HGRN_HEAD_V = D_MODEL // HGRN_HEADS
HGRN_KEY_DIM = HGRN_HEADS * HGRN_EXPAND
HGRN_VAL_DIM = HGRN_HEADS * HGRN_HEAD_V
HGRN_IN_DIM = 2 * HGRN_KEY_DIM + 2 * HGRN_VAL_DIM

D_FF = -(-8 * D_MODEL // (3 * 256)) * 256
CHUNK = 64
EPS = 1e-6

kernel_name = "hybrid_gdn_hgrn2_trunk"


def rmsnorm(x, w, eps=EPS):
    xf = x.astype(jnp.float32)
    y = xf * lax.rsqrt(jnp.mean(xf * xf, axis=-1, keepdims=True) + eps)
    return (y * w.astype(jnp.float32)).astype(x.dtype)


def l2norm(x, eps=1e-6):
    xf = x.astype(jnp.float32)
    return xf * lax.rsqrt(jnp.sum(xf * xf, axis=-1, keepdims=True) + eps)


def causal_depthwise_conv(x, w):
    K, C = w.shape
    return lax.conv_general_dilated(
        x, w[:, None, :].astype(x.dtype), window_strides=(1,), padding=[(K - 1, 0)],
        dimension_numbers=("NWC", "WIO", "NWC"), feature_group_count=C)


def to_chunks(t):
    B, S, H, d = t.shape
    return t.transpose(0, 2, 1, 3).reshape(B, H, S // CHUNK, CHUNK, d)


def from_chunks(o):
    N, B, H, C, d = o.shape
    return jnp.moveaxis(o, 0, 2).reshape(B, H, N * C, d).transpose(0, 2, 1, 3)


def gated_delta_rule_chunked(q, k, v, beta, g):
    qc, kc, vc = to_chunks(q), to_chunks(k), to_chunks(v.astype(jnp.float32))
    bc = to_chunks(beta[..., None])[..., 0]
    G = jnp.cumsum(to_chunks(g[..., None])[..., 0], axis=-1)
    causal = jnp.tril(jnp.ones((CHUNK, CHUNK), bool))
    strict = jnp.tril(jnp.ones((CHUNK, CHUNK), bool), k=-1)
    decay = jnp.exp(jnp.where(causal, G[..., :, None] - G[..., None, :], -jnp.inf))
    kb = kc * bc[..., None]
    A = jnp.where(strict, jnp.einsum("bhnid,bhnjd->bhnij", kb, kc) * decay, 0.0)
    eye = jnp.eye(CHUNK, dtype=jnp.float32)
    T = lax.linalg.triangular_solve(eye + A, jnp.broadcast_to(eye, A.shape), left_side=True,
                                    lower=True, unit_diagonal=True)
    u = jnp.einsum("bhnij,bhnje->bhnie", T, vc * bc[..., None])
    w = jnp.einsum("bhnij,bhnjd->bhnid", T, kb * jnp.exp(G)[..., None])
    a_qk = jnp.einsum("bhnid,bhnjd->bhnij", qc, kc) * decay
    q_dec = qc * jnp.exp(G)[..., None]
    k_dec = kc * jnp.exp(G[..., -1:] - G)[..., None]
    g_last = jnp.exp(G[..., -1])

    def step(S, inp):
        u_c, w_c, aqk_c, qd_c, kd_c, gl_c = inp
        v_new = u_c - jnp.einsum("bhid,bhde->bhie", w_c, S)
        o = jnp.einsum("bhid,bhde->bhie", qd_c, S) + jnp.einsum("bhij,bhje->bhie", aqk_c, v_new)
        S = S * gl_c[..., None, None] + jnp.einsum("bhid,bhie->bhde", kd_c, v_new)
        return S, o

    B, H = qc.shape[0], qc.shape[1]
    S0 = jnp.zeros((B, H, qc.shape[-1], vc.shape[-1]), jnp.float32)
    xs = tuple(jnp.moveaxis(t, 2, 0) for t in (u, w, a_qk, q_dec, k_dec, g_last))
    _, o = lax.scan(step, S0, xs)
    return from_chunks(o)


def hgrn2_chunked(q, k, v, logf):
    qc, kc, vc = to_chunks(q), to_chunks(k), to_chunks(v.astype(jnp.float32))
    Bc = jnp.cumsum(to_chunks(logf), axis=-2)
    q_dec = qc * jnp.exp(Bc)
    k_dec = kc * jnp.exp(Bc[..., -1:, :] - Bc)
    f_last = jnp.exp(Bc[..., -1, :])
    causal = jnp.tril(jnp.ones((CHUNK, CHUNK), bool))[:, :, None]

    def step(S, inp):
        q_c, k_c, v_c, b_c, qd_c, kd_c, fl_c = inp
        dec = jnp.exp(jnp.where(causal, b_c[..., :, None, :] - b_c[..., None, :, :], -jnp.inf))
        scores = jnp.einsum("bhid,bhjd,bhijd->bhij", q_c, k_c, dec)
        o = jnp.einsum("bhid,bhde->bhie", qd_c, S) + jnp.einsum("bhij,bhje->bhie", scores, v_c)
        S = S * fl_c[..., :, None] + jnp.einsum("bhjd,bhje->bhde", kd_c, v_c)
        return S, o

    B, H = qc.shape[0], qc.shape[1]
    S0 = jnp.zeros((B, H, qc.shape[-1], vc.shape[-1]), jnp.float32)
    xs = tuple(jnp.moveaxis(t, 2, 0) for t in (qc, kc, vc, Bc, q_dec, k_dec, f_last))
    _, o = lax.scan(step, S0, xs)
    return from_chunks(o)


def gdn_mixer(h, w_in, conv_w, a_log, dt_bias, head_norm, w_out):
    B, S, _ = h.shape
    proj = h @ w_in
    qkv = jax.nn.silu(causal_depthwise_conv(proj[..., :GDN_CONV_DIM], conv_w))
    z = proj[..., GDN_CONV_DIM:GDN_CONV_DIM + GDN_VAL_DIM]
    b = proj[..., GDN_CONV_DIM + GDN_VAL_DIM:GDN_CONV_DIM + GDN_VAL_DIM + GDN_V_HEADS]
    a = proj[..., GDN_CONV_DIM + GDN_VAL_DIM + GDN_V_HEADS:]
    q = l2norm(qkv[..., :GDN_KEY_DIM].reshape(B, S, GDN_K_HEADS, GDN_HEAD_K)) * (GDN_HEAD_K ** -0.5)
    k = l2norm(qkv[..., GDN_KEY_DIM:2 * GDN_KEY_DIM].reshape(B, S, GDN_K_HEADS, GDN_HEAD_K))
    v = qkv[..., 2 * GDN_KEY_DIM:].reshape(B, S, GDN_V_HEADS, GDN_HEAD_V)
    rep = GDN_V_HEADS // GDN_K_HEADS
    q = jnp.repeat(q, rep, axis=2)
    k = jnp.repeat(k, rep, axis=2)
    beta = jax.nn.sigmoid(b.astype(jnp.float32))
    g = -jnp.exp(a_log.astype(jnp.float32)) * jax.nn.softplus(a.astype(jnp.float32) + dt_bias.astype(jnp.float32))
    o = gated_delta_rule_chunked(q, k, v, beta, g).astype(h.dtype)
    o = rmsnorm(o, head_norm) * jax.nn.silu(z.reshape(B, S, GDN_V_HEADS, GDN_HEAD_V))
    return o.reshape(B, S, GDN_VAL_DIM) @ w_out


def hgrn2_mixer(h, w_in, lower_bound, head_norm, w_out):
    B, S, _ = h.shape
    proj = h @ w_in
    q = jax.nn.silu(proj[..., :HGRN_KEY_DIM].astype(jnp.float32)).reshape(B, S, HGRN_HEADS, HGRN_EXPAND)
    fl = proj[..., HGRN_KEY_DIM:2 * HGRN_KEY_DIM].astype(jnp.float32).reshape(B, S, HGRN_HEADS, HGRN_EXPAND)
    i = proj[..., 2 * HGRN_KEY_DIM:2 * HGRN_KEY_DIM + HGRN_VAL_DIM].reshape(B, S, HGRN_HEADS, HGRN_HEAD_V)
    gate = proj[..., 2 * HGRN_KEY_DIM + HGRN_VAL_DIM:].reshape(B, S, HGRN_HEADS, HGRN_HEAD_V)
    lb = lower_bound.astype(jnp.float32).reshape(HGRN_HEADS, HGRN_EXPAND)
    logf = jnp.logaddexp(jnp.log(lb), jnp.log1p(-lb) + jax.nn.log_sigmoid(fl))
    k = (1.0 - lb) * jax.nn.sigmoid(-fl)
    o = hgrn2_chunked(q, k, i, logf).astype(h.dtype)
    o = rmsnorm(o, head_norm) * jax.nn.silu(gate)
    return o.reshape(B, S, HGRN_VAL_DIM) @ w_out


def swiglu(h, w_gate, w_up, w_down):
    return (jax.nn.silu(h @ w_gate) * (h @ w_up)) @ w_down


def setup_inputs(seed: int = 0) -> dict:
    key = jax.random.key(seed)
    ks = jax.random.split(key, 20)
    f32 = jnp.float32

    def nrm(k, shape, scale):
        return jax.random.normal(k, shape, f32) * scale

    def gain(k, shape):
        return 1.0 + 0.02 * jax.random.normal(k, shape, f32)

    dt = jnp.exp(jax.random.uniform(ks[5], (N_GDN_LAYERS, GDN_V_HEADS), f32, math.log(1e-3), math.log(1e-1)))
    return {
        "x": nrm(ks[0], (BATCH, SEQ, D_MODEL), 1.0),
        "gdn_norm": gain(ks[1], (N_GDN_LAYERS, D_MODEL)),
        "gdn_w_in": nrm(ks[2], (N_GDN_LAYERS, D_MODEL, GDN_IN_DIM), D_MODEL ** -0.5),
        "gdn_conv": nrm(ks[3], (N_GDN_LAYERS, GDN_CONV, GDN_CONV_DIM), GDN_CONV ** -0.5),
        "gdn_a_log": jnp.log(jax.random.uniform(ks[4], (N_GDN_LAYERS, GDN_V_HEADS), f32, 1.0, 16.0)),
        "gdn_dt_bias": dt + jnp.log(-jnp.expm1(-dt)),
        "gdn_head_norm": gain(ks[6], (N_GDN_LAYERS, GDN_HEAD_V)),
        "gdn_w_out": nrm(ks[7], (N_GDN_LAYERS, GDN_VAL_DIM, D_MODEL), GDN_VAL_DIM ** -0.5),
        "hgrn_norm": gain(ks[8], (N_HGRN_LAYERS, D_MODEL)),
        "hgrn_w_in": nrm(ks[9], (N_HGRN_LAYERS, D_MODEL, HGRN_IN_DIM), D_MODEL ** -0.5),
        "hgrn_lower_bounds": nrm(ks[10], (DEPTH, HGRN_KEY_DIM), 0.5),
        "hgrn_head_norm": gain(ks[11], (N_HGRN_LAYERS, HGRN_HEAD_V)),
        "hgrn_w_out": nrm(ks[12], (N_HGRN_LAYERS, HGRN_VAL_DIM, D_MODEL), HGRN_VAL_DIM ** -0.5),
        "ffn_norm": gain(ks[13], (DEPTH, D_MODEL)),
        "ffn_w_gate": nrm(ks[14], (DEPTH, D_MODEL, D_FF), D_MODEL ** -0.5),
        "ffn_w_up": nrm(ks[15], (DEPTH, D_MODEL, D_FF), D_MODEL ** -0.5),
        "ffn_w_down": nrm(ks[16], (DEPTH, D_FF, D_MODEL), D_FF ** -0.5),
        "final_norm": gain(ks[17], (D_MODEL,)),
    }


def reference(x, gdn_norm, gdn_w_in, gdn_conv, gdn_a_log, gdn_dt_bias, gdn_head_norm, gdn_w_out,
              hgrn_norm, hgrn_w_in, hgrn_lower_bounds, hgrn_head_norm, hgrn_w_out,
              ffn_norm, ffn_w_gate, ffn_w_up, ffn_w_down, final_norm):
    lb_table = jnp.cumsum(jax.nn.softmax(hgrn_lower_bounds.astype(jnp.float32), axis=0), axis=0)
    lb_table = lb_table - lb_table[0]
    h = x
    for layer in range(DEPTH):
        j = layer // N_MIXERS
        if layer % N_MIXERS == 0:
            h = h + gdn_mixer(rmsnorm(h, gdn_norm[j]), gdn_w_in[j], gdn_conv[j], gdn_a_log[j],
                              gdn_dt_bias[j], gdn_head_norm[j], gdn_w_out[j])
        else:
            h = h + hgrn2_mixer(rmsnorm(h, hgrn_norm[j]), hgrn_w_in[j], lb_table[layer],
                                hgrn_head_norm[j], hgrn_w_out[j])
        h = h + swiglu(rmsnorm(h, ffn_norm[layer]), ffn_w_gate[layer], ffn_w_up[layer], ffn_w_down[layer])
    return rmsnorm(h, final_norm)
```

```python
import numpy as np
import ml_dtypes
from contextlib import ExitStack
import concourse.bass as bass
import concourse.mybir as mybir
from concourse.bass_utils import run_bass_kernel_spmd

F32 = mybir.dt.float32
BF16 = mybir.dt.bfloat16
AF = mybir.ActivationFunctionType
ALU = mybir.AluOpType
AX = mybir.AxisListType

D = 2048
SEQ = 8192
NCORE = 8
NT = SEQ // NCORE
DFF = 5632
EPS = 1e-6


class Buf:
    __slots__ = ("name", "w", "r", "sem", "cnt", "excl")

    def __init__(self, name):
        self.name = name
        self.excl = False
        self.w = None
        self.r = {}
        self.sem = None
        self.cnt = 0


class Prog:
    ENG = ("pe", "act", "dve", "pool", "sp")

    def __init__(self, nc, es):
        self.nc = nc
        self.es = es
        self.q = {e: [] for e in self.ENG}
        self.sem = {e: es.enter_context(nc.semaphore("s_" + e)) for e in self.ENG}
        self.cnt = {e: 0 for e in self.ENG}
        self.known = {e: {} for e in self.ENG}
        self.nbuf = 0
        self.all_sems = [self.sem[e] for e in self.ENG]

    def sb(self, name, shape, dtype):
        return self.es.enter_context(self.nc.sbuf_tensor("sb_" + name, shape, dtype))

    def ps(self, name, shape, dtype):
        return self.es.enter_context(self.nc.psum_tensor("ps_" + name, shape, dtype))

    def buf(self, name=None):
        self.nbuf += 1
        return Buf(name or ("b%d" % self.nbuf))

    def _waits(self, eng, reads, writes):
        need = {}

        def add(d):
            if d is None:
                return
            s, v = d
            k = id(s)
            if k not in need or need[k][1] < v:
                need[k] = (s, v)

        for b in reads:
            add(b.w)
        for b in writes:
            add(b.w)
            for d in b.r.values():
                add(d)
        out = []
        kn = self.known[eng]
        for k, (s, v) in need.items():
            if eng == "pe" and s is self.sem["pe"]:
                continue
            if kn.get(k, 0) >= v:
                continue
            kn[k] = v
            out.append((s, v))
        return out

    def op(self, eng, fn, reads=(), writes=(), track=None):
        if any(b.excl for b in reads):
            writes = tuple(writes) + tuple(b for b in reads if b.excl)
            reads = tuple(b for b in reads if not b.excl)
        waits = self._waits(eng, reads, writes)
        if track is not None:
            if track.sem is None:
                track.sem = self.es.enter_context(self.nc.semaphore("d_" + track.name))
                self.all_sems.append(track.sem)
            track.cnt += 16
            done = (track.sem, track.cnt)
            inc = 16
        else:
            self.cnt[eng] += 1
            done = (self.sem[eng], self.cnt[eng])
            inc = 1
        k = id(done[0])
        for b in reads:
            if k not in b.r or b.r[k][1] < done[1]:
                b.r[k] = done
        for b in writes:
            b.w = done
            b.r = {}
        self.q[eng].append((waits, fn, done[0], inc))

    def wait_all(self, eng, bufs):
        waits = self._waits(eng, bufs, ())
        self.q[eng].append((waits, None, None, 0))

    def emit(self):
        nc = self.nc

        def mk(name):
            def f(e):
                for waits, fn, sem, inc in self.q[name]:
                    for (s, v) in waits:
                        e.wait_ge(s, v)
                    if fn is not None:
                        fn(e).then_inc(sem, inc)
            return f

        sems = list(self.all_sems)

        def clr(e):
            for s in sems:
                e.sem_clear(s)

        with nc.Block() as b0:
            b0.sync(clr)
        with nc.Block() as block:
            block.tensor(mk("pe"))
            block.scalar(mk("act"))
            block.vector(mk("dve"))
            block.gpsimd(mk("pool"))
            block.sync(mk("sp"))


def dma(P, q, out_ap, in_ap, reads, writes, track):
    P.op(q, lambda e: e.dma_start(out=out_ap, in_=in_ap), reads, writes, track=track)


def mm(P, out_ap, lhsT, rhs, start, stop, reads, writes):
    P.op("pe", lambda e: e.matmul(out_ap, lhsT, rhs, start=start, stop=stop), reads, writes)


def ACT(P, out, in_, func, reads, writes, scale=None, bias=None, accum=None):
    kw = {}
    if scale is not None:
        kw["scale"] = scale
    if bias is not None:
        kw["bias"] = bias
    if accum is not None:
        kw["accum_out"] = accum
    P.op("act", lambda e: e.activation(out, in_, func, **kw), reads, writes)


def TS(P, eng, out, in0, s1, s2, op0, op1, reads, writes):
    if op1 is None:
        P.op(eng, lambda e: e.tensor_scalar(out, in0, s1, s2, op0), reads, writes)
    else:
        P.op(eng, lambda e: e.tensor_scalar(out, in0, s1, s2, op0, op1), reads, writes)


def TT(P, eng, out, in0, in1, op, reads, writes):
    P.op(eng, lambda e: e.tensor_tensor(out, in0, in1, op), reads, writes)


def STT(P, eng, out, in0, sc, in1, op0, op1, reads, writes):
    P.op(eng, lambda e: e.scalar_tensor_tensor(out, in0, sc, in1, op0, op1), reads, writes)


def CP(P, eng, out, in_, reads, writes):
    if eng == "act":
        P.op(eng, lambda e: e.copy(out, in_), reads, writes)
    else:
        P.op(eng, lambda e: e.tensor_copy(out, in_), reads, writes)


def TR(P, out, in_, ident, reads, writes):
    P.op("pe", lambda e: e.transpose(out, in_, ident), reads, writes)


class Consts:
    pass


def make_consts(P):
    C = Consts()
    C.ones_f = P.sb("ones_f", [128, 128], F32)
    C.ident_f = P.sb("ident_f", [128, 128], F32)
    C.b_ones = P.buf("ones")
    C.b_ident = P.buf("ident")
    P.op("pool", lambda e: e.memset(C.ones_f[:], 1.0), (), (C.b_ones,))
    P.op("pool", lambda e: e.affine_select(C.ident_f[:], C.ones_f[:], [[-1, 128]], ALU.is_equal, 0.0,
                                           base=0, channel_multiplier=1), (C.b_ones,), (C.b_ident,))
    return C


class TState:
    pass


def alloc_T(P):
    S = TState()
    NTT = NT // 128
    S.H = [P.sb("H%d" % t, [128, D], F32) for t in range(NTT)]
    S.bH = [P.buf("H%d" % t) for t in range(NTT)]
    S.U = [P.sb("U%d" % j, [128, NT], BF16) for j in range(22)]
    S.bU = [P.buf("U%d" % j) for j in range(22)]
    S.hnT = P.sb("hnT", [128, 16, NT], BF16)
    S.bhnT = [P.buf("hnT%d" % t) for t in range(NTT)]
    S.wgu = [[P.sb("wgu%d_%d" % (i, m), [128, 16, 256], BF16) for m in range(2)] for i in range(2)]
    S.bwgu = [[P.buf("wgu%d_%d" % (i, m)) for m in range(2)] for i in range(2)]
    S.NRING = 4
    S.wr = [P.sb("wr%d" % i, [128, 512], BF16) for i in range(S.NRING)]
    S.bwr = [P.buf("wr%d" % i) for i in range(S.NRING)]
    S.ring_i = 0
    S.XS = P.sb("XS", [128, 1024], F32)
    S.bXS = P.buf("XS")
    S.junk = P.sb("junk", [128, D], BF16)
    S.bjunk = P.buf("junk")
    S.sg = [P.sb("sg%d" % i, [128, 512], BF16) for i in range(2)]
    S.bsg = [P.buf("sg%d" % i) for i in range(2)]
    S.ss = P.sb("ss", [128, 8], F32)
    S.bss = [P.buf("ss%d" % t) for t in range(8)]
    S.rstd = P.sb("rstd", [128, 8], F32)
    S.brstd = [P.buf("rstd%d" % t) for t in range(8)]
    S.g = P.sb("gvec", [128, 16], F32)
    S.bg = P.buf("gvec")
    S.PS = [P.ps("PS%d" % i, [128, 512], F32) for i in range(8)]
    S.bPS = [P.buf("PS%d" % i) for i in range(8)]
    return S


def rms_stats(P, S, t):
    ACT(P, S.junk[:], S.H[t][:], AF.Square, (S.bH[t],), (S.bjunk, S.bss[t]), accum=S.ss[:, t:t + 1])
    TS(P, "dve", S.rstd[:, t:t + 1], S.ss[:, t:t + 1], 1.0 / D, EPS, ALU.mult, ALU.add, (S.bss[t],), (S.brstd[t],))
    ACT(P, S.rstd[:, t:t + 1], S.rstd[:, t:t + 1], AF.Sqrt, (S.brstd[t],), (S.brstd[t],))
    P.op("dve", lambda e: e.reciprocal(S.rstd[:, t:t + 1], S.rstd[:, t:t + 1]), (S.brstd[t],), (S.brstd[t],))


def norm_to_featmajor(P, S, C, g_dram, bg_dram):
    NTT = NT // 128
    dma(P, "sp", S.g[:], g_dram, (bg_dram,), (S.bg,), S.bg)
    for t in range(NTT):
        rms_stats(P, S, t)
        for hh in range(2):
            ACT(P, S.XS[:], S.H[t][:, hh * 1024:(hh + 1) * 1024], AF.Copy, (S.bH[t], S.brstd[t]), (S.bXS,),
                scale=S.rstd[:, t:t + 1])
            for q4 in range(2):
                pb = (t * 4 + hh * 2 + q4) % 8
                for i in range(4):
                    kk = q4 * 4 + i
                    TR(P, S.PS[pb][:, i * 128:(i + 1) * 128], S.XS[:, kk * 128:(kk + 1) * 128], C.ident_f[:],
                       (S.bXS, C.b_ident), (S.bPS[pb],))
                for i in range(4):
                    k = hh * 8 + q4 * 4 + i
                    TS(P, "dve", S.hnT[:, k, t * 128:(t + 1) * 128], S.PS[pb][:, i * 128:(i + 1) * 128],
                       S.g[:, k:k + 1], None, ALU.mult, None, (S.bPS[pb], S.bg), (S.bhnT[t],))


def ring_load(P, S, src_ap, bsrc):
    i = S.ring_i % S.NRING
    S.ring_i += 1
    dma(P, "pool", S.wr[i][:], src_ap, (bsrc,), (S.bwr[i],), S.bwr[i])
    return i


def proj_accumulate(P, S, nk, w_dram, bw, krow0):
    NTT = NT // 128
    for n in range(D // 512):
        for k in range(nk):
            i = ring_load(P, S, w_dram[krow0 + k * 128: krow0 + (k + 1) * 128, n * 512:(n + 1) * 512], bw)
            for t in range(NTT):
                mm(P, S.PS[t][:], S.U[k][:, t * 128:(t + 1) * 128], S.wr[i][:], k == 0, k == nk - 1,
                   (S.bU[k], S.bwr[i]), (S.bPS[t],))
        for t in range(NTT):
            TT(P, "dve", S.H[t][:, n * 512:(n + 1) * 512], S.PS[t][:], S.H[t][:, n * 512:(n + 1) * 512], ALU.add,
               (S.bPS[t], S.bH[t]), (S.bH[t],))


def phase_T(P, S, C, dr, DV, final):
    NTT = NT // 128
    nkc = DV // 128
    for kb in range(0, nkc, 16):
        nk = min(16, nkc - kb)
        for k in range(nk):
            dma(P, "sp", S.U[k][:], dr["oT"][(kb + k) * 128:(kb + k + 1) * 128, :], (dr["b_oT"],), (S.bU[k],), S.bU[k])
        proj_accumulate(P, S, nk, dr["w_out"], dr["b_w"], kb * 128)
    stage = dr.get("stage", 99)
    if stage <= 1:
        for t in range(NTT):
            dma(P, "sp", dr["h_out"][t * 128:(t + 1) * 128, :], S.H[t][:], (S.bH[t],), (dr["b_hout"],), dr["b_hout"])
        P.wait_all("sp", [dr["b_hout"]])
        return
    norm_to_featmajor(P, S, C, dr["g_ffn"], dr["b_w"])
    if stage <= 2:
        for k in range(16):
            dma(P, "sp", dr["hnT_out"][k * 128:(k + 1) * 128, :], S.hnT[:, k, :], tuple(S.bhnT), (dr["b_hnout"],),
                dr["b_hnout"])
        P.wait_all("sp", [dr["b_hnout"]])
        return
    wg_v = dr["w_gate"].rearrange("(k p) c -> p k c", p=128)
    wu_v = dr["w_up"].rearrange("(k p) c -> p k c", p=128)
    grp = 0
    for hf in range(2):
        for jj in range(11):
            c0 = (hf * 22 + jj * 2) * 128
            sl = grp % 2
            grp += 1
            dma(P, "pool", S.wgu[sl][0][:], wg_v[:, :, c0:c0 + 256], (dr["b_w"],), (S.bwgu[sl][0],), S.bwgu[sl][0])
            dma(P, "pool", S.wgu[sl][1][:], wu_v[:, :, c0:c0 + 256], (dr["b_w"],), (S.bwgu[sl][1],), S.bwgu[sl][1])
            for j in range(2):
                jc = jj * 2 + j
                pset = (jc % 2) * 4
                for k in range(16):
                    for m in range(2):
                        for t2 in range(2):
                            pb = pset + m * 2 + t2
                            mm(P, S.PS[pb][:], S.wgu[sl][m][:, k, j * 128:(j + 1) * 128],
                               S.hnT[:, k, t2 * 512:(t2 + 1) * 512], k == 0, k == 15,
                               (S.bwgu[sl][m],) + tuple(S.bhnT[t2 * 4:(t2 + 1) * 4]), (S.bPS[pb],))
                for t2 in range(2):
                    pg = pset + t2
                    pu = pset + 2 + t2
                    ACT(P, S.sg[t2][:], S.PS[pg][:], AF.Silu, (S.bPS[pg],), (S.bsg[t2],))
                    TT(P, "dve", S.U[jc][:, t2 * 512:(t2 + 1) * 512], S.PS[pu][:], S.sg[t2][:], ALU.mult,
                       (S.bsg[t2], S.bPS[pu]), (S.bU[jc],))
        if stage == 3:
            for k in range(16):
                dma(P, "sp", dr["hnT_out"][k * 128:(k + 1) * 128, :], S.U[k][:], (S.bU[k],), (dr["b_hnout"],),
                    dr["b_hnout"])
            P.wait_all("sp", [dr["b_hnout"]])
            return
        proj_accumulate(P, S, 22, dr["w_down"], dr["b_w"], hf * 22 * 128)
        if stage == 4:
            for t in range(NTT):
                dma(P, "sp", dr["h_out"][t * 128:(t + 1) * 128, :], S.H[t][:], (S.bH[t],), (dr["b_hout"],), dr["b_hout"])
            P.wait_all("sp", [dr["b_hout"]])
            return
    if not final:
        for t in range(NTT):
            dma(P, "sp", dr["h_out"][t * 128:(t + 1) * 128, :], S.H[t][:], (S.bH[t],), (dr["b_hout"],), dr["b_hout"])
        norm_to_featmajor(P, S, C, dr["g_next"], dr["b_w"])
        for k in range(16):
            dma(P, "sp", dr["hnT_out"][k * 128:(k + 1) * 128, :], S.hnT[:, k, :], tuple(S.bhnT), (dr["b_hnout"],),
                dr["b_hnout"])
        P.wait_all("sp", [dr["b_hout"], dr["b_hnout"]])
    else:
        for t in range(NTT):
            rms_stats(P, S, t)
        for hh in range(2):
            dma(P, "sp", S.XS[:], dr["g_final_b"][:, hh * 1024:(hh + 1) * 1024], (dr["b_w"],), (S.bXS,), S.bXS)
            for t in range(NTT):
                STT(P, "dve", S.H[t][:, hh * 1024:(hh + 1) * 1024], S.H[t][:, hh * 1024:(hh + 1) * 1024],
                    S.rstd[:, t:t + 1], S.XS[:], ALU.mult, ALU.mult, (S.bH[t], S.brstd[t], S.bXS), (S.bH[t],))
        for t in range(NTT):
            dma(P, "sp", dr["out"][t * 128:(t + 1) * 128, :], S.H[t][:], (S.bH[t],), (dr["b_out"],), dr["b_out"])
        P.wait_all("sp", [dr["b_out"]])


def build_T(DV, final, stage=99):
    nc = bass.Bass("TRN2", target_bir_lowering=False)
    es = ExitStack()
    dr = {}
    dr["h_in"] = nc.dram_tensor("h_in", [NT, D], F32, kind="ExternalInput").ap()
    dr["oT"] = nc.dram_tensor("oT", [DV, NT], BF16, kind="ExternalInput").ap()
    dr["w_out"] = nc.dram_tensor("w_out", [DV, D], F32, kind="ExternalInput").ap()
    dr["g_ffn"] = nc.dram_tensor("g_ffn", [128, 16], F32, kind="ExternalInput").ap()
    dr["w_gate"] = nc.dram_tensor("w_gate", [D, DFF], F32, kind="ExternalInput").ap()
    dr["w_up"] = nc.dram_tensor("w_up", [D, DFF], F32, kind="ExternalInput").ap()
    dr["w_down"] = nc.dram_tensor("w_down", [DFF, D], F32, kind="ExternalInput").ap()
    if final:
        dr["g_final_b"] = nc.dram_tensor("g_final_b", [128, D], F32, kind="ExternalInput").ap()
        dr["out"] = nc.dram_tensor("out", [NT, D], F32, kind="ExternalOutput").ap()
    else:
        dr["g_next"] = nc.dram_tensor("g_next", [128, 16], F32, kind="ExternalInput").ap()
        dr["h_out"] = nc.dram_tensor("h_out", [NT, D], F32, kind="ExternalOutput").ap()
        dr["hnT_out"] = nc.dram_tensor("hnT_out", [D, NT], BF16, kind="ExternalOutput").ap()
    dr["stage"] = stage
    with es:
        P = Prog(nc, es)
        for nm in ("b_oT", "b_w", "b_hin", "b_hout", "b_hnout", "b_out"):
            dr[nm] = P.buf(nm)
        C = make_consts(P)
        S = alloc_T(P)
        for t in range(NT // 128):
            dma(P, "sp", S.H[t][:], dr["h_in"][t * 128:(t + 1) * 128, :], (dr["b_hin"],), (S.bH[t],), S.bH[t])
        phase_T(P, S, C, dr, DV, final)
        P.emit()
    return nc


def build_N():
    nc = bass.Bass("TRN2", target_bir_lowering=False)
    es = ExitStack()
    dr = {}
    dr["h_in"] = nc.dram_tensor("h_in", [NT, D], F32, kind="ExternalInput").ap()
    dr["g_next"] = nc.dram_tensor("g_next", [128, 16], F32, kind="ExternalInput").ap()
    dr["hnT_out"] = nc.dram_tensor("hnT_out", [D, NT], BF16, kind="ExternalOutput").ap()
    with es:
        P = Prog(nc, es)
        for nm in ("b_w", "b_hin", "b_hnout"):
            dr[nm] = P.buf(nm)
        C = make_consts(P)
        S = alloc_T(P)
        for t in range(NT // 128):
            dma(P, "sp", S.H[t][:], dr["h_in"][t * 128:(t + 1) * 128, :], (dr["b_hin"],), (S.bH[t],), S.bH[t])
        norm_to_featmajor(P, S, C, dr["g_next"], dr["b_w"])
        for k in range(16):
            dma(P, "sp", dr["hnT_out"][k * 128:(k + 1) * 128, :], S.hnT[:, k, :], tuple(S.bhnT), (dr["b_hnout"],),
                dr["b_hnout"])
        P.wait_all("sp", [dr["b_hnout"]])
        P.emit()
    return nc


class Pool2:
    def __init__(self, P, name, n, shape, dtype, psum=False, banks=None):
        self.n = n
        self.i = 0
        if psum:
            self.t = []
            self.b = []
            per = 512 // shape[1]
            for bk in banks:
                for s in range(per):
                    self.t.append(bk[:shape[0], s * shape[1]:(s + 1) * shape[1]])
                    self.b.append(P.buf("%s%d" % (name, len(self.b))))
            self.n = len(self.t)
        else:
            T = P.sb(name, [shape[0], n, shape[1]], dtype)
            self.t = [T[:, i, :] for i in range(n)]
            self.b = [P.buf("%s%d" % (name, i)) for i in range(n)]

    def get(self):
        i = self.i % self.n
        self.i += 1
        return self.t[i], self.b[i]


def rsqrt_ops(P, out, in_, addc, mulc, rd, wr):
    TS(P, "dve", out, in_, mulc, addc, ALU.mult, ALU.add, rd, wr)
    ACT(P, out, out, AF.Sqrt, wr, wr)
    P.op("dve", lambda e: e.reciprocal(out, out), wr, wr)


class PsumPool:
    def __init__(self, P, name, banks):
        self.banks = banks
        self.bufs = [P.buf("%s%d" % (name, i)) for i in range(len(banks))]
        for b in self.bufs:
            b.excl = True
        self.i = 0

    def align(self):
        self.i = (self.i + 3) // 4 * 4

    def get(self):
        b = (self.i // 4) % len(self.banks)
        s_ = self.i % 4
        self.i += 1
        return self.banks[b][:, s_ * 128:(s_ + 1) * 128], self.bufs[b]


def groups4(seq):
    seq = list(seq)
    return [seq[i:i + 4] for i in range(0, len(seq), 4)]


def emit_out_norm4(P, C, outs, gain_ap, bgain, f128, sm_p, psC):
    psC.align()
    slots = []
    for (o, bo, szT_ap, bsz, oT_ap, boT) in outs:
        sm, bsm = sm_p.get()
        junk, bjunk = f128.get()
        ACT(P, junk, o, AF.Square, (bo,), (bjunk, bsm), accum=sm[:, 0:1])
        rsqrt_ops(P, sm[:, 1:2], sm[:, 0:1], 1e-6, 1.0 / 128, (bsm,), (bsm,))
        STT(P, "dve", o, o, sm[:, 1:2], gain_ap, ALU.mult, ALU.mult, (bo, bsm, bgain), (bo,))
    for (o, bo, szT_ap, bsz, oT_ap, boT) in outs:
        p5, b5 = psC.get()
        TR(P, p5[:, 0:64], o, C.ident_f[:64, :64], (bo, C.b_ident), (b5,))
        slots.append((p5, b5))
    for (o, bo, szT_ap, bsz, oT_ap, boT), (p5, b5) in zip(outs, slots):
        TT(P, "dve", oT_ap, p5[:, 0:64], szT_ap, ALU.mult, (b5, bsz), (boT,))
    psC.align()


def mixer_common(P):
    C = make_consts(P)
    C.ident_b = P.sb("ident_b", [128, 128], BF16)
    C.b_identb = P.buf("identb")
    CP(P, "dve", C.ident_b[:], C.ident_f[:], (C.b_ident,), (C.b_identb,))
    C.tri = P.sb("tri", [64, 64], F32)
    C.b_tri = P.buf("tri")
    C.mst = P.sb("mst", [64, 64], F32)
    C.b_mst = P.buf("mst")
    P.op("pool", lambda e: e.affine_select(C.tri[:], C.ones_f[:64, :64], [[1, 64]], ALU.is_ge, 0.0, base=0,
                                           channel_multiplier=-1), (C.b_ones,), (C.b_tri,))
    P.op("pool", lambda e: e.affine_select(C.mst[:], C.ones_f[:64, :64], [[-1, 64]], ALU.is_gt, 0.0, base=0,
                                           channel_multiplier=1), (C.b_ones,), (C.b_mst,))
    return C


def build_G(S_len=SEQ, dbg=99):
    nc = bass.Bass("TRN2", target_bir_lowering=False)
    es = ExitStack()
    hnT_d = nc.dram_tensor("hnT", [D, S_len], BF16, kind="ExternalInput").ap()
    w_d = nc.dram_tensor("w", [D, 1536], F32, kind="ExternalInput").ap()
    wba_d = nc.dram_tensor("wba", [D, 8], F32, kind="ExternalInput").ap()
    cw_d = nc.dram_tensor("cw", [128, 32], F32, kind="ExternalInput").ap()
    par_d = nc.dram_tensor("par", [128, 192], F32, kind="ExternalInput").ap()
    oT_d = nc.dram_tensor("oT", [512, S_len], BF16, kind="ExternalOutput").ap()
    NTILE = S_len // 512
    with es:
        P = Prog(nc, es)
        bin_ = P.buf("in")
        bout = P.buf("out")
        C = mixer_common(P)
        tri, b_tri, mst, b_mst = C.tri, C.b_tri, C.mst, C.b_mst
        W = P.sb("W", [128, 16, 1536], BF16); bW = P.buf("W")
        wv = w_d.rearrange("(k p) c -> p k c", p=128)
        for kq in range(4):
            dma(P, "pool", W[:, kq * 4:(kq + 1) * 4, :], wv[:, kq * 4:(kq + 1) * 4, :], (bin_,), (bW,), bW)
        Wba = P.sb("Wba", [128, 16, 8], BF16); bWba = P.buf("Wba")
        dma(P, "pool", Wba[:], wba_d.rearrange("(k p) c -> p k c", p=128), (bin_,), (bWba,), bWba)
        cw = P.sb("cw", [128, 32], F32); bcw = P.buf("cw")
        dma(P, "sp", cw[:], cw_d, (bin_,), (bcw,), bcw)
        par = P.sb("par", [128, 192], F32); bpar = P.buf("par")
        dma(P, "sp", par[:], par_d, (bin_,), (bpar,), bpar)
        negA = P.sb("negA", [128, 32], F32); bnegA = P.buf("negA")
        ACT(P, negA[:], par[:, 0:32], AF.Exp, (bpar,), (bnegA,))
        TS(P, "dve", negA[:], negA[:], -1.0, None, ALU.mult, None, (bnegA,), (bnegA,))
        Sf = [P.sb("Sf%d" % h, [128, 128], F32) for h in range(4)]
        bSf = [P.buf("Sf%d" % h) for h in range(4)]
        Sb = [[P.sb("Sb%d_%d" % (h, i), [128, 128], BF16) for i in range(2)] for h in range(4)]
        bSb = [[P.buf("Sb%d_%d" % (h, i)) for i in range(2)] for h in range(4)]
        for h in range(4):
            P.op("pool", lambda e, h=h: e.memset(Sf[h][:], 0.0), (), (bSf[h],))
            P.op("pool", lambda e, h=h: e.memset(Sb[h][0][:], 0.0), (), (bSb[h][0],))
        X = P.sb("X", [128, 16, 512], BF16); bX = P.buf("X")
        pre = [P.sb("pre%d" % ct, [128, 515], F32) for ct in range(8)]
        bpre = [P.buf("pre%d" % ct) for ct in range(8)]
        for ct in range(8):
            P.op("pool", lambda e, ct=ct: e.memset(pre[ct][:, 0:3], 0.0), (), (bpre[ct],))
        acc = [P.sb("acc%d" % ct, [128, 512], F32) for ct in range(8)]
        bacc = [P.buf("acc%d" % ct) for ct in range(8)]
        a16 = [P.sb("a16_%d" % ct, [128, 512], BF16) for ct in range(8)]
        ba16 = [P.buf("a16_%d" % ct) for ct in range(8)]
        sz = [P.sb("sz%d" % h, [128, 512], BF16) for h in range(4)]
        bsz = [P.buf("sz%d" % h) for h in range(4)]
        tmpA = P.sb("tmpA", [128, 512], F32); btmpA = P.buf("tmpA")
        tmpB = P.sb("tmpB", [128, 512], F32); btmpB = P.buf("tmpB")
        ktok = P.sb("ktok", [64, 16, 128], BF16)
        bktok = [P.buf("ktok%d" % i) for i in range(16)]
        vtok = P.sb("vtok", [64, 32, 128], BF16)
        bvtok = [P.buf("vtok%d" % i) for i in range(32)]
        GN = ("beta", "nbeta", "g", "G", "expG", "kdsc", "bexpG")
        gt = {nm: P.sb("gt_" + nm, [64, 32], F32) for nm in GN}
        gl = P.sb("gt_gl", [128, 32], F32)
        bgate = P.buf("gates")
        oTt = [P.sb("oTt%d" % h, [128, 512], BF16) for h in range(4)]
        boTt = [P.buf("oTt%d" % h) for h in range(4)]
        PSb = [P.ps("PSb%d" % i, [128, 512], F32) for i in range(8)]
        bGA = [P.buf("GA0"), P.buf("GA1")]
        bF = P.buf("F2")
        for b_ in bGA + [bF]:
            b_.excl = True
        FB = PSb[2]
        psA = PsumPool(P, "psAB", [PSb[3], PSb[4]])
        psB = psA
        psC = PsumPool(P, "psC", [PSb[5], PSb[6], PSb[7]])
        NI = 16
        m64 = {nm: Pool2(P, nm, NI, [64, 64], BF16) for nm in ("Pa", "Pb", "PTa", "PTb", "TTa", "TTb")}
        m64["Em"] = Pool2(P, "Em", 8, [64, 64], F32)
        m64["ETm"] = Pool2(P, "ETm", 8, [64, 64], F32)
        dg_p = Pool2(P, "dg4", 2, [64, 256], F32)
        aq_p = Pool2(P, "aqkT", NI, [64, 64], BF16)
        u_p = Pool2(P, "u", NI, [64, 128], F32)
        wT_p = Pool2(P, "wT", NI, [128, 64], BF16)
        kd_p = Pool2(P, "kd", NI, [64, 128], BF16)
        b128 = Pool2(P, "b128", 8, [64, 128], BF16)
        f128 = Pool2(P, "f128", 8, [64, 128], F32)
        vn_p = Pool2(P, "vn", 8, [64, 128], BF16)
        o_p = Pool2(P, "o", 8, [64, 128], F32)
        ssc_p = Pool2(P, "ssc", 4, [128, 128], F32)
        sm_p = Pool2(P, "sm", 16, [64, 2], F32)

        hv = hnT_d.rearrange("(k p) t -> p k t", p=128)
        for ti in range(NTILE):
            t0 = ti * 512
            dma(P, "sp", X[:], hv[:, :, t0:t0 + 512], (bin_,), (bX,), bX)
            for ct in range(12):
                ga = ct % 2
                for k in range(16):
                    mm(P, PSb[ga][:], W[:, k, ct * 128:(ct + 1) * 128], X[:, k, :], k == 0, k == 15,
                       (bW, bX), (bGA[ga],))
                if ct < 8:
                    CP(P, "act", pre[ct][:, 3:515], PSb[ga][:], (bGA[ga],), (bpre[ct],))
                    TS(P, "dve", acc[ct][:], pre[ct][:, 3:515], cw[:, ct * 4 + 3:ct * 4 + 4], None, ALU.mult, None,
                       (bpre[ct], bcw), (bacc[ct],))
                    for tap in range(3):
                        STT(P, "dve", acc[ct][:], pre[ct][:, tap:tap + 512], cw[:, ct * 4 + tap:ct * 4 + tap + 1],
                            acc[ct][:], ALU.mult, ALU.add, (bpre[ct], bcw, bacc[ct]), (bacc[ct],))
                    CP(P, "pool", pre[ct][:, 0:3], pre[ct][:, 512:515], (bpre[ct],), (bpre[ct],))
                    if ct < 4:
                        ACT(P, acc[ct][:], acc[ct][:], AF.Silu, (bacc[ct],), (bacc[ct],))
                    else:
                        ACT(P, a16[ct][:], acc[ct][:], AF.Silu, (bacc[ct],), (ba16[ct],))
                else:
                    ACT(P, sz[ct - 8][:], PSb[ga][:], AF.Silu, (bGA[ga],), (bsz[ct - 8],))
            for ct in range(4):
                TT(P, "pool", tmpA[:], acc[ct][:], acc[ct][:], ALU.mult, (bacc[ct],), (btmpA,))
                ga = ct % 2
                mm(P, PSb[ga][:], C.ones_f[:], tmpA[:], True, True, (C.b_ones, btmpA), (bGA[ga],))
                rsqrt_ops(P, tmpB[:], PSb[ga][:], 1e-6, 1.0, (bGA[ga],), (btmpB,))
                if ct < 2:
                    STT(P, "dve", a16[ct][:], acc[ct][:], 128.0 ** -0.5, tmpB[:], ALU.mult, ALU.mult,
                        (bacc[ct], btmpB), (ba16[ct],))
                else:
                    TT(P, "dve", a16[ct][:], acc[ct][:], tmpB[:], ALU.mult, (bacc[ct], btmpB), (ba16[ct],))
            if dbg == 0:
                break
            jobs = []
            for c in range(8):
                for hk in range(2):
                    jobs.append((2 + hk, c, ktok[:, hk * 8 + c, :], bktok[hk * 8 + c]))
                for h in range(4):
                    jobs.append((4 + h, c, vtok[:, h * 8 + c, :], bvtok[h * 8 + c]))
            for gi, grp in enumerate(groups4(jobs)):
                psA.align()
                sl = []
                for (ct, c, dst, bdst) in grp:
                    pt, pbf = psA.get()
                    mm(P, pt[:64, :], a16[ct][:, c * 64:(c + 1) * 64], C.ident_b[:], True, True,
                       (ba16[ct], C.b_identb), (pbf,))
                    sl.append((pt, pbf))
                for (ct, c, dst, bdst), (pt, pbf) in zip(grp, sl):
                    CP(P, "act" if gi % 2 == 0 else "dve", dst, pt[:64, :], (pbf,), (bdst,))
            psA.align()
            if dbg == 1:
                break
            psA.align()
            pt, pbf = psA.get()
            psA.align()
            for c in range(8):
                for k in range(16):
                    mm(P, pt[:64, c * 8:(c + 1) * 8], X[:, k, c * 64:(c + 1) * 64], Wba[:, k, :], k == 0, k == 15,
                       (bX, bWba), (pbf,))
            ba3 = pt[:64, 0:64].rearrange("p (c x) -> p c x", x=8)
            v3 = lambda ap: ap.rearrange("p (c h) -> p c h", h=4)
            ACT(P, v3(gt["beta"][:]), ba3[:, :, 0:4], AF.Sigmoid, (pbf,), (bgate,))
            TT(P, "dve", v3(gt["g"][:]), ba3[:, :, 4:8], v3(par[:64, 32:64]), ALU.add, (pbf, bpar), (bgate,))
            ACT(P, gt["g"][:], gt["g"][:], AF.Exp, (bgate,), (bgate,))
            TS(P, "dve", gt["g"][:], gt["g"][:], 1.0, None, ALU.add, None, (bgate,), (bgate,))
            ACT(P, gt["g"][:], gt["g"][:], AF.Ln, (bgate,), (bgate,))
            TT(P, "dve", gt["g"][:], gt["g"][:], negA[:64, :], ALU.mult, (bgate, bnegA), (bgate,))
            TS(P, "dve", gt["nbeta"][:], gt["beta"][:], -1.0, None, ALU.mult, None, (bgate,), (bgate,))
            mm(P, FB[:64, 0:32], tri[:], gt["g"][:], True, True, (b_tri, bgate), (bF,))
            mm(P, FB[:, 32:64], C.ones_f[:64, :], gt["g"][:], True, True, (C.b_ones, bgate), (bF,))
            CP(P, "dve", gt["G"][:], FB[:64, 0:32], (bF,), (bgate,))
            ACT(P, gt["expG"][:], FB[:64, 0:32], AF.Exp, (bF,), (bgate,))
            ACT(P, gl[:], FB[:, 32:64], AF.Exp, (bF,), (bgate,))
            TT(P, "dve", gt["kdsc"][:], FB[:64, 32:64], gt["G"][:], ALU.subtract, (bF, bgate), (bgate,))
            ACT(P, gt["kdsc"][:], gt["kdsc"][:], AF.Exp, (bgate,), (bgate,))
            TT(P, "dve", gt["bexpG"][:], gt["beta"][:], gt["expG"][:], ALU.mult, (bgate,), (bgate,))
            if dbg == 2:
                break

            item = {}

            def rec_chunk(c, ti=ti, item=item):
                par_i = (ti * 8 + c) % 2
                R = {h: {} for h in range(4)}
                psC.align()
                for h in range(4):
                    s_ = item[(h, c)]
                    p1, b1 = psC.get()
                    mm(P, p1[:64, :], s_["wT"], Sb[h][par_i][:], True, True, (s_["bwT"], bSb[h][par_i]), (b1,))
                    R[h].update(ws=p1, bws=b1)
                for h in range(4):
                    p2, b2 = psC.get()
                    mm(P, p2[:64, :], a16[h // 2][:, c * 64:(c + 1) * 64], Sb[h][par_i][:], True, True,
                       (ba16[h // 2], bSb[h][par_i]), (b2,))
                    R[h].update(qs=p2, bqs=b2)
                for h in range(4):
                    s_ = item[(h, c)]
                    vn, bvn = vn_p.get()
                    STT(P, "dve", vn, R[h]["ws"][:64, :], -1.0, s_["u"], ALU.mult, ALU.add, (R[h]["bws"], s_["bu"]), (bvn,))
                    R[h].update(vn=vn, bvn=bvn)
                for h in range(4):
                    s_ = item[(h, c)]
                    p3, b3 = psC.get()
                    mm(P, p3[:64, :], s_["aq"], R[h]["vn"], True, True, (s_["baq"], R[h]["bvn"]), (b3,))
                    R[h].update(av=p3, bav=b3)
                for h in range(4):
                    s_ = item[(h, c)]
                    p4, b4 = psC.get()
                    mm(P, p4[:, :], s_["kd"], R[h]["vn"], True, True, (s_["bkd"], R[h]["bvn"]), (b4,))
                    R[h].update(kv=p4, bkv=b4)
                for h in range(4):
                    ci = c * 4 + h
                    avs, bavs = f128.get()
                    CP(P, "act", avs, R[h]["av"][:64, :], (R[h]["bav"],), (bavs,))
                    o, bo = o_p.get()
                    STT(P, "dve", o, R[h]["qs"][:64, :], gt["expG"][:, ci:ci + 1], avs, ALU.mult, ALU.add,
                        (R[h]["bqs"], bgate, bavs), (bo,))
                    R[h].update(o=o, bo=bo)
                for h in range(4):
                    ci = c * 4 + h
                    ssc, bssc = ssc_p.get()
                    ACT(P, ssc, Sf[h][:], AF.Copy, (bSf[h], bgate), (bssc,), scale=gl[:, ci:ci + 1])
                    TT(P, "dve", Sf[h][:], R[h]["kv"][:, :], ssc, ALU.add, (R[h]["bkv"], bssc), (bSf[h],))
                    CP(P, "act", Sb[h][1 - par_i][:], Sf[h][:], (bSf[h],), (bSb[h][1 - par_i],))
                emit_out_norm4(P, C, [(R[h]["o"], R[h]["bo"], sz[h][:, c * 64:(c + 1) * 64], bsz[h],
                                       oTt[h][:, c * 64:(c + 1) * 64], boTt[h]) for h in range(4)],
                               par[:64, 64:192], bpar, f128, sm_p, psC)

            for half in range(2):
                items = [(h, c) for c in range(half * 4, half * 4 + 4) for h in range(4)]
                st = {}
                for cp in range(2):
                    cs = [half * 4 + cp * 2, half * 4 + cp * 2 + 1]
                    ems = {}
                    for c in cs:
                        dg, bdg = dg_p.get()
                        for h in range(4):
                            ci = c * 4 + h
                            TS(P, "dve", dg[:, h * 64:(h + 1) * 64], C.ident_f[:64, :64], gt["G"][:, ci:ci + 1], None,
                               ALU.mult, None, (C.b_ident, bgate), (bdg,))
                        mm(P, FB[:64, 0:256], C.ones_f[:64, :64], dg, True, True, (C.b_ones, bdg), (bF,))
                        for h in range(4):
                            ci = c * 4 + h
                            gcol = gt["G"][:, ci:ci + 1]
                            Em, bEm = m64["Em"].get()
                            ETm, bETm = m64["ETm"].get()
                            TS(P, "dve", Em, FB[:64, h * 64:(h + 1) * 64], gcol, 0.0, ALU.subtract, ALU.max, (bF, bgate), (bEm,))
                            TS(P, "dve", ETm, FB[:64, h * 64:(h + 1) * 64], gcol, 0.0, ALU.subtract, ALU.min, (bF, bgate), (bETm,))
                            ACT(P, Em, Em, AF.Exp, (bEm,), (bEm,), scale=-1.0)
                            ACT(P, ETm, ETm, AF.Exp, (bETm,), (bETm,))
                            TT(P, "pool", Em, Em, mst[:], ALU.mult, (bEm, b_mst), (bEm,))
                            TT(P, "pool", ETm, ETm, tri[:], ALU.mult, (bETm, b_tri), (bETm,))
                            ems[(h, c)] = (Em, bEm, ETm, bETm)
                    psB.align()
                    kk = {}
                    for c in cs:
                        for hk in range(2):
                            kT = a16[2 + hk][:, c * 64:(c + 1) * 64]
                            qT = a16[hk][:, c * 64:(c + 1) * 64]
                            pt, pbf = psB.get()
                            mm(P, pt[:64, 0:64], kT, kT, True, True, (ba16[2 + hk],), (pbf,))
                            mm(P, pt[:64, 64:128], kT, qT, True, True, (ba16[2 + hk], ba16[hk]), (pbf,))
                            kk[(hk, c)] = (pt, pbf)
                    for c in cs:
                        for h in range(4):
                            ci = c * 4 + h
                            Em, bEm, ETm, bETm = ems[(h, c)]
                            pt, pbf = kk[(h // 2, c)]
                            Pa, bPa = m64["Pa"].get()
                            STT(P, "dve", Pa, pt[:64, 0:64], gt["nbeta"][:, ci:ci + 1], Em, ALU.mult, ALU.mult,
                                (pbf, bgate, bEm), (bPa,))
                            aq, baq = aq_p.get()
                            TT(P, "dve", aq, pt[:64, 64:128], ETm, ALU.mult, (pbf, bETm), (baq,))
                            st[(h, c)] = dict(P=Pa, bP=bPa, aq=aq, baq=baq)
                if dbg == 3:
                    break
                for grp in groups4(items):
                    psB.align()
                    sl = {}
                    for it in grp:
                        s_ = st[it]
                        pt, pbf = psB.get()
                        mm(P, pt[:64, 0:64], s_["P"], C.ident_b[:64, :64], True, True, (s_["bP"], C.b_identb), (pbf,))
                        sl[it] = (pt, pbf)
                    for it in grp:
                        s_ = st[it]
                        pt, pbf = sl[it]
                        PT, bPT = m64["PTa"].get()
                        TTm, bTT = m64["TTa"].get()
                        CP(P, "act", PT, pt[:64, 0:64], (pbf,), (bPT,))
                        TT(P, "dve", TTm, pt[:64, 0:64], C.ident_f[:64, :64], ALU.add, (pbf, C.b_ident), (bTT,))
                        s_.update(PT=PT, bPT=bPT, TT=TTm, bTT=bTT)
                for lvl in range(5):
                    nP = "Pb" if lvl % 2 == 0 else "Pa"
                    nPT = "PTb" if lvl % 2 == 0 else "PTa"
                    nTT = "TTb" if lvl % 2 == 0 else "TTa"
                    for grp in groups4(items):
                        psB.align()
                        sl = {}
                        for it in grp:
                            s_ = st[it]
                            pt, pbf = psB.get()
                            mm(P, pt[:64, 0:64], s_["PT"], s_["P"], True, True, (s_["bPT"], s_["bP"]), (pbf,))
                            if lvl < 4:
                                mm(P, pt[:64, 64:128], s_["P"], s_["PT"], True, True, (s_["bPT"], s_["bP"]), (pbf,))
                            sl[it] = (pt, pbf)
                        for it in grp:
                            s_ = st[it]
                            pt, pbf = sl[it]
                            Pn, bPn = m64[nP].get()
                            CP(P, "act", Pn, pt[:64, 0:64], (pbf,), (bPn,))
                            if lvl < 4:
                                PTn, bPTn = m64[nPT].get()
                                CP(P, "dve", PTn, pt[:64, 64:128], (pbf,), (bPTn,))
                                s_.update(PT=PTn, bPT=bPTn)
                            s_.update(P=Pn, bP=bPn)
                    for grp in groups4(items):
                        psB.align()
                        sl = {}
                        for it in grp:
                            s_ = st[it]
                            pt, pbf = psB.get()
                            mm(P, pt[:64, 0:64], s_["P"], s_["TT"], True, True, (s_["bP"], s_["bTT"]), (pbf,))
                            sl[it] = (pt, pbf)
                        for it in grp:
                            s_ = st[it]
                            pt, pbf = sl[it]
                            TTn, bTTn = m64[nTT].get()
                            TT(P, "dve", TTn, pt[:64, 0:64], s_["TT"], ALU.add, (pbf, s_["bTT"]), (bTTn,))
                            s_.update(TT=TTn, bTT=bTTn)
                if dbg == 4:
                    break
                for grp in [items[i:i + 2] for i in range(0, len(items), 2)]:
                    psB.align()
                    sl = {}
                    for (h, c) in grp:
                        s_ = st[(h, c)]
                        ci = c * 4 + h
                        hk = h // 2
                        vb, bvb = b128.get()
                        TS(P, "pool", vb, vtok[:, h * 8 + c, :], gt["beta"][:, ci:ci + 1], None, ALU.mult, None,
                           (bvtok[h * 8 + c], bgate), (bvb,))
                        kbg, bkbg = b128.get()
                        TS(P, "pool", kbg, ktok[:, hk * 8 + c, :], gt["bexpG"][:, ci:ci + 1], None, ALU.mult, None,
                           (bktok[hk * 8 + c], bgate), (bkbg,))
                        kd, bkd = kd_p.get()
                        TS(P, "pool", kd, ktok[:, hk * 8 + c, :], gt["kdsc"][:, ci:ci + 1], None, ALU.mult, None,
                           (bktok[hk * 8 + c], bgate), (bkd,))
                        pt, pbf = psB.get()
                        mm(P, pt[:64, :], s_["TT"], vb, True, True, (s_["bTT"], bvb), (pbf,))
                        pt2, pbf2 = psB.get()
                        mm(P, pt2[:, 0:64], kbg, s_["TT"], True, True, (s_["bTT"], bkbg), (pbf2,))
                        sl[(h, c)] = (pt, pbf, pt2, pbf2)
                        s_.update(kd=kd, bkd=bkd)
                    for (h, c) in grp:
                        s_ = st[(h, c)]
                        pt, pbf, pt2, pbf2 = sl[(h, c)]
                        u, bu = u_p.get()
                        CP(P, "act", u, pt[:64, :], (pbf,), (bu,))
                        wT, bwT = wT_p.get()
                        CP(P, "dve", wT, pt2[:, 0:64], (pbf2,), (bwT,))
                        s_.update(u=u, bu=bu, wT=wT, bwT=bwT)
                psB.align()
                item.update(st)
                if dbg == 5:
                    break
                for c in range(half * 4, half * 4 + 4):
                    rec_chunk(c)
            if dbg < 99:
                break
            for h in range(4):
                dma(P, "sp", oT_d[h * 128:(h + 1) * 128, t0:t0 + 512], oTt[h][:], (boTt[h],), (bout,), boTt[h])
        if dbg < 99:
            for h in range(4):
                dma(P, "sp", oT_d[h * 128:(h + 1) * 128, 0:512], oTt[h][:], (boTt[h],), (bout,), boTt[h])
        P.wait_all("sp", boTt)
        P.emit()
    return nc

def build_H(S_len=SEQ):
    nc = bass.Bass("TRN2", target_bir_lowering=False)
    es = ExitStack()
    hnT_d = nc.dram_tensor("hnT", [D, S_len], BF16, kind="ExternalInput").ap()
    w_d = nc.dram_tensor("w", [D, 1024], F32, kind="ExternalInput").ap()
    lbp_d = nc.dram_tensor("lbp", [128, 4], F32, kind="ExternalInput").ap()
    hng_d = nc.dram_tensor("hng", [128, 128], F32, kind="ExternalInput").ap()
    oT_d = nc.dram_tensor("oT", [256, S_len], BF16, kind="ExternalOutput").ap()
    NTILE = S_len // 512
    with es:
        P = Prog(nc, es)
        bin_ = P.buf("in")
        bout = P.buf("out")
        C = mixer_common(P)
        tri, b_tri = C.tri, C.b_tri
        W = P.sb("W", [128, 16, 1024], BF16); bW = P.buf("W")
        wv = w_d.rearrange("(k p) c -> p k c", p=128)
        for kq in range(4):
            dma(P, "pool", W[:, kq * 4:(kq + 1) * 4, :], wv[:, kq * 4:(kq + 1) * 4, :], (bin_,), (bW,), bW)
        lbp = P.sb("lbp", [128, 4], F32); blbp = P.buf("lbp")
        dma(P, "sp", lbp[:], lbp_d, (bin_,), (blbp,), blbp)
        hng = P.sb("hng", [128, 128], F32); bhng = P.buf("hng")
        dma(P, "sp", hng[:], hng_d, (bin_,), (bhng,), bhng)
        lb = P.sb("lb", [128, 4], F32); blb = P.buf("lb")
        TT(P, "dve", lb[:, 0:2], lbp[:, 2:4], lbp[:, 0:2], ALU.subtract, (blbp,), (blb,))
        ACT(P, lb[:, 2:4], lb[:, 0:2], AF.Sigmoid, (blb,), (blb,), scale=-1.0)
        ACT(P, lb[:, 0:2], lb[:, 0:2], AF.Sigmoid, (blb,), (blb,))
        Sf = [P.sb("Sf%d" % h, [128, 128], F32) for h in range(2)]
        bSf = [P.buf("Sf%d" % h) for h in range(2)]
        Sb = [[P.sb("Sb%d_%d" % (h, i), [128, 128], BF16) for i in range(2)] for h in range(2)]
        bSb = [[P.buf("Sb%d_%d" % (h, i)) for i in range(2)] for h in range(2)]
        for h in range(2):
            P.op("pool", lambda e, h=h: e.memset(Sf[h][:], 0.0), (), (bSf[h],))
            P.op("pool", lambda e, h=h: e.memset(Sb[h][0][:], 0.0), (), (bSb[h][0],))
        X = P.sb("X", [128, 16, 512], BF16); bX = P.buf("X")
        f32t = {nm: [P.sb("%s%d" % (nm, h), [128, 512], F32) for h in range(2)] for nm in ("qf", "fv", "kf", "Bt", "eB", "ek")}
        bf32 = {nm: [P.buf("%s%d" % (nm, h)) for h in range(2)] for nm in f32t}
        b16t = {nm: [P.sb("%s%d" % (nm, h), [128, 512], BF16) for h in range(2)] for nm in ("qd", "kh", "v16", "szh")}
        bb16 = {nm: [P.buf("%s%d" % (nm, h)) for h in range(2)] for nm in b16t}
        vtok = P.sb("vtokH", [64, 16, 128], BF16); bvtok = [P.buf("vtokH%d" % i) for i in range(16)]
        ktok = P.sb("ktokH", [64, 16, 128], BF16); bktok = [P.buf("ktokH%d" % i) for i in range(16)]
        oTt = [P.sb("oTt%d" % h, [128, 512], BF16) for h in range(2)]
        boTt = [P.buf("oTt%d" % h) for h in range(2)]
        PSb = [P.ps("PSb%d" % i, [128, 512], F32) for i in range(8)]
        bGA = [P.buf("GA0"), P.buf("GA1")]
        for b_ in bGA:
            b_.excl = True
        psA = PsumPool(P, "psA", [PSb[2], PSb[3]])
        psC = PsumPool(P, "psC", [PSb[4], PSb[5], PSb[6], PSb[7]])
        sc_p = Pool2(P, "sc", 4, [64, 64], BF16)
        f128 = Pool2(P, "f128", 8, [64, 128], F32)
        o_p = Pool2(P, "o", 4, [64, 128], F32)
        ssc_p = Pool2(P, "ssc", 2, [128, 128], F32)
        sm_p = Pool2(P, "sm", 8, [64, 2], F32)

        hv = hnT_d.rearrange("(k p) t -> p k t", p=128)
        for ti in range(NTILE):
            t0 = ti * 512
            dma(P, "sp", X[:], hv[:, :, t0:t0 + 512], (bin_,), (bX,), bX)
            for ct in range(8):
                ga = ct % 2
                hk = ct % 2
                for k in range(16):
                    mm(P, PSb[ga][:], W[:, k, ct * 128:(ct + 1) * 128], X[:, k, :], k == 0, k == 15, (bW, bX), (bGA[ga],))
                if ct < 2:
                    ACT(P, f32t["qf"][hk][:], PSb[ga][:], AF.Silu, (bGA[ga],), (bf32["qf"][hk],))
                elif ct < 4:
                    fv, bfv = f32t["fv"][hk], bf32["fv"][hk]
                    ACT(P, fv[:], PSb[ga][:], AF.Sigmoid, (bGA[ga],), (bfv,))
                    TS(P, "dve", fv[:], fv[:], lb[:, 2 + hk:3 + hk], lb[:, hk:hk + 1], ALU.mult, ALU.add, (bfv, blb), (bfv,))
                    TS(P, "dve", f32t["kf"][hk][:], fv[:], -1.0, 1.0, ALU.mult, ALU.add, (bfv,), (bf32["kf"][hk],))
                    ACT(P, fv[:], fv[:], AF.Ln, (bfv,), (bfv,))
                elif ct < 6:
                    CP(P, "act", b16t["v16"][hk][:], PSb[ga][:], (bGA[ga],), (bb16["v16"][hk],))
                else:
                    ACT(P, b16t["szh"][hk][:], PSb[ga][:], AF.Silu, (bGA[ga],), (bb16["szh"][hk],))
            for hk in range(2):
                Bt, bBt = f32t["Bt"][hk], bf32["Bt"][hk]
                for c in range(8):
                    P.op("dve", lambda e, hk=hk, c=c: e.tensor_tensor_scan(
                        f32t["Bt"][hk][:, c * 64:(c + 1) * 64], C.ones_f[:, 0:64], f32t["fv"][hk][:, c * 64:(c + 1) * 64],
                        0.0, ALU.mult, ALU.add), (bf32["fv"][hk], C.b_ones), (bBt,))
                ACT(P, f32t["eB"][hk][:], Bt[:], AF.Exp, (bBt,), (bf32["eB"][hk],))
                TT(P, "dve", b16t["qd"][hk][:], f32t["qf"][hk][:], f32t["eB"][hk][:], ALU.mult,
                   (bf32["qf"][hk], bf32["eB"][hk]), (bb16["qd"][hk],))
                TS(P, "dve", f32t["ek"][hk][:], Bt[:], -80.0, None, ALU.max, None, (bBt,), (bf32["ek"][hk],))
                ACT(P, f32t["ek"][hk][:], f32t["ek"][hk][:], AF.Exp, (bf32["ek"][hk],), (bf32["ek"][hk],), scale=-1.0)
                TT(P, "dve", b16t["kh"][hk][:], f32t["kf"][hk][:], f32t["ek"][hk][:], ALU.mult,
                   (bf32["kf"][hk], bf32["ek"][hk]), (bb16["kh"][hk],))
            jobs = []
            for c in range(8):
                for hk in range(2):
                    jobs.append((b16t["v16"][hk], bb16["v16"][hk], c, vtok[:, hk * 8 + c, :], bvtok[hk * 8 + c]))
                    jobs.append((b16t["kh"][hk], bb16["kh"][hk], c, ktok[:, hk * 8 + c, :], bktok[hk * 8 + c]))
            for gi, grp in enumerate(groups4(jobs)):
                psA.align()
                sl = []
                for (src, bsrc, c, dst, bdst) in grp:
                    pt, pbf = psA.get()
                    mm(P, pt[:64, :], src[:, c * 64:(c + 1) * 64], C.ident_b[:], True, True, (bsrc, C.b_identb), (pbf,))
                    sl.append((pt, pbf))
                for (src, bsrc, c, dst, bdst), (pt, pbf) in zip(grp, sl):
                    CP(P, "act" if gi % 2 == 0 else "dve", dst, pt[:64, :], (pbf,), (bdst,))
            psA.align()
            for c in range(8):
                par_i = (ti * 8 + c) % 2
                cs = slice(c * 64, (c + 1) * 64)
                R = {0: {}, 1: {}}
                psC.align()
                for hk in range(2):
                    p1, b1 = psC.get()
                    mm(P, p1[:64, 0:64], b16t["kh"][hk][:, cs], b16t["qd"][hk][:, cs], True, True,
                       (bb16["kh"][hk], bb16["qd"][hk]), (b1,))
                    p2, b2 = psC.get()
                    mm(P, p2[:64, :], b16t["qd"][hk][:, cs], Sb[hk][par_i][:], True, True,
                       (bb16["qd"][hk], bSb[hk][par_i]), (b2,))
                    R[hk].update(sc=p1, bsc=b1, qs=p2, bqs=b2)
                for hk in range(2):
                    scm, bscm = sc_p.get()
                    TT(P, "dve", scm, R[hk]["sc"][:64, 0:64], tri[:], ALU.mult, (R[hk]["bsc"], b_tri), (bscm,))
                    o, bo = o_p.get()
                    CP(P, "act", o, R[hk]["qs"][:64, :], (R[hk]["bqs"],), (bo,))
                    R[hk].update(scm=scm, bscm=bscm, o=o, bo=bo)
                psC.align()
                for hk in range(2):
                    p3, b3 = psC.get()
                    mm(P, p3[:64, :], R[hk]["scm"], vtok[:, hk * 8 + c, :], True, True, (R[hk]["bscm"], bvtok[hk * 8 + c]), (b3,))
                    p4, b4 = psC.get()
                    mm(P, p4[:, :], ktok[:, hk * 8 + c, :], vtok[:, hk * 8 + c, :], True, True,
                       (bktok[hk * 8 + c], bvtok[hk * 8 + c]), (b4,))
                    R[hk].update(sv=p3, bsv=b3, kv=p4, bkv=b4)
                for hk in range(2):
                    o, bo = R[hk]["o"], R[hk]["bo"]
                    TT(P, "dve", o, R[hk]["sv"][:64, :], o, ALU.add, (R[hk]["bsv"], bo), (bo,))
                    fl = f32t["eB"][hk][:, c * 64 + 63:c * 64 + 64]
                    ssc, bssc = ssc_p.get()
                    ACT(P, ssc, Sf[hk][:], AF.Copy, (bSf[hk], bf32["eB"][hk]), (bssc,), scale=fl)
                    STT(P, "dve", Sf[hk][:], R[hk]["kv"][:, :], fl, ssc, ALU.mult, ALU.add,
                        (R[hk]["bkv"], bf32["eB"][hk], bssc), (bSf[hk],))
                    CP(P, "act", Sb[hk][1 - par_i][:], Sf[hk][:], (bSf[hk],), (bSb[hk][1 - par_i],))
                emit_out_norm4(P, C, [(R[hk]["o"], R[hk]["bo"], b16t["szh"][hk][:, cs], bb16["szh"][hk],
                                       oTt[hk][:, cs], boTt[hk]) for hk in range(2)], hng[:64, :], bhng, f128, sm_p, psC)
            for h in range(2):
                dma(P, "sp", oT_d[h * 128:(h + 1) * 128, t0:t0 + 512], oTt[h][:], (boTt[h],), (bout,), boTt[h])
        P.wait_all("sp", boTt)
        P.emit()
    return nc


def _gvec(g):
    return np.ascontiguousarray(np.asarray(g, np.float32).reshape(16, 128).T)


def _host_G(c, w_in, conv, a_log, dt_bias, head_norm):
    qs = slice(256 * c, 256 * c + 256)
    ks = slice(2048 + 256 * c, 2048 + 256 * c + 256)
    vs = slice(4096 + 512 * c, 4096 + 512 * c + 512)
    zs = slice(8192 + 512 * c, 8192 + 512 * c + 512)
    bs = slice(12288 + 4 * c, 12288 + 4 * c + 4)
    as_ = slice(12320 + 4 * c, 12320 + 4 * c + 4)
    w = np.concatenate([w_in[:, qs], w_in[:, ks], w_in[:, vs], w_in[:, zs]], axis=1)
    wba = np.concatenate([w_in[:, bs], w_in[:, as_]], axis=1)
    cwc = np.concatenate([conv[:, qs], conv[:, ks], conv[:, vs]], axis=1)
    cw = cwc.reshape(4, 8, 128).transpose(2, 1, 0).reshape(128, 32)
    par = np.concatenate([np.tile(a_log[4 * c:4 * c + 4], 8), np.tile(dt_bias[4 * c:4 * c + 4], 8), head_norm])
    par = np.broadcast_to(par[None, :], (128, 192))
    return dict(w=np.ascontiguousarray(w, np.float32), wba=np.ascontiguousarray(wba, np.float32),
                cw=np.ascontiguousarray(cw, np.float32), par=np.ascontiguousarray(par, np.float32))


def _host_H(c, w_in, lbs, head_norm):
    sl = lambda base: slice(base + 256 * c, base + 256 * c + 256)
    w = np.concatenate([w_in[:, sl(0)], w_in[:, sl(2048)], w_in[:, sl(4096)], w_in[:, sl(6144)]], axis=1)
    l = lbs[:, 256 * c:256 * c + 256]
    lbp = np.stack([l[0, :128], l[0, 128:], l[1, :128], l[1, 128:]], axis=1)
    return dict(w=np.ascontiguousarray(w, np.float32), lbp=np.ascontiguousarray(lbp, np.float32),
                hng=np.ascontiguousarray(np.broadcast_to(head_norm, (128, 128)), np.float32))


_NC_CACHE = {}
_DBG = {}


def _nc(key, fn):
    if key not in _NC_CACHE:
        _NC_CACHE[key] = fn()
    return _NC_CACHE[key]


def kernel(x, gdn_norm, gdn_w_in, gdn_conv, gdn_a_log, gdn_dt_bias, gdn_head_norm, gdn_w_out,
           hgrn_norm, hgrn_w_in, hgrn_lower_bounds, hgrn_head_norm, hgrn_w_out,
           ffn_norm, ffn_w_gate, ffn_w_up, ffn_w_down, final_norm):
    f = lambda a: np.asarray(a, np.float32)
    x2 = np.ascontiguousarray(f(x).reshape(SEQ, D))
    cores = list(range(NCORE))
    tok = lambda c: slice(c * NT, (c + 1) * NT)
    ffn_norm, ffn_w_gate, ffn_w_up, ffn_w_down = f(ffn_norm), f(ffn_w_gate), f(ffn_w_up), f(ffn_w_down)
    res = run_bass_kernel_spmd(_nc("N", build_N), [{"h_in": x2[tok(c)], "g_next": _gvec(f(gdn_norm)[0])} for c in cores],
                               core_ids=cores)
    hnT = np.ascontiguousarray(np.concatenate([r["hnT_out"] for r in res.results], axis=1))
    _DBG["hn0T"] = hnT
    gw, gc = f(gdn_w_in)[0], f(gdn_conv)[0]
    maps = []
    for c in cores:
        m = _host_G(c, gw, gc, f(gdn_a_log)[0], f(gdn_dt_bias)[0], f(gdn_head_norm)[0])
        m["hnT"] = hnT
        maps.append(m)
    res = run_bass_kernel_spmd(_nc("G", build_G), maps, core_ids=cores)
    oT = np.concatenate([r["oT"] for r in res.results], axis=0)
    _DBG["o0T"] = oT
    maps = [{"h_in": x2[tok(c)], "oT": np.ascontiguousarray(oT[:, tok(c)]), "w_out": f(gdn_w_out)[0],
             "g_ffn": _gvec(ffn_norm[0]), "w_gate": ffn_w_gate[0], "w_up": ffn_w_up[0], "w_down": ffn_w_down[0],
             "g_next": _gvec(f(hgrn_norm)[0])} for c in cores]
    res = run_bass_kernel_spmd(_nc("T0", lambda: build_T(4096, False)), maps, core_ids=cores)
    h1 = [r["h_out"] for r in res.results]
    hnT = np.ascontiguousarray(np.concatenate([r["hnT_out"] for r in res.results], axis=1))
    _DBG["h1"] = h1
    _DBG["hn1T"] = hnT
    hw = f(hgrn_w_in)[0]
    maps = []
    for c in cores:
        m = _host_H(c, hw, f(hgrn_lower_bounds), f(hgrn_head_norm)[0])
        m["hnT"] = hnT
        maps.append(m)
    res = run_bass_kernel_spmd(_nc("H", build_H), maps, core_ids=cores)
    oT = np.concatenate([r["oT"] for r in res.results], axis=0)
    _DBG["o1T"] = oT
    gfb = np.ascontiguousarray(np.broadcast_to(f(final_norm)[None, :], (128, D)))
    maps = [{"h_in": h1[c], "oT": np.ascontiguousarray(oT[:, tok(c)]), "w_out": f(hgrn_w_out)[0],
             "g_ffn": _gvec(ffn_norm[1]), "w_gate": ffn_w_gate[1], "w_up": ffn_w_up[1], "w_down": ffn_w_down[1],
             "g_final_b": gfb} for c in cores]
    res = run_bass_kernel_spmd(_nc("T1", lambda: build_T(2048, True)), maps, core_ids=cores)
    out = np.concatenate([r["out"] for r in res.results], axis=0)
    return np.ascontiguousarray(out.reshape(1, SEQ, D).astype(np.float32))
```

```python
import numpy as np
import ml_dtypes
from contextlib import ExitStack
import concourse.bass as bass
import concourse.mybir as mybir
from concourse.bass_utils import run_bass_kernel_spmd

F32 = mybir.dt.float32
BF16 = mybir.dt.bfloat16
AF = mybir.ActivationFunctionType
ALU = mybir.AluOpType
AX = mybir.AxisListType

D = 2048
SEQ = 8192
NCORE = 8
NT = SEQ // NCORE
DFF = 5632
EPS = 1e-6


class Buf:
    __slots__ = ("name", "w", "r", "sem", "cnt", "excl")

    def __init__(self, name):
        self.name = name
        self.excl = False
        self.w = None
        self.r = {}
        self.sem = None
        self.cnt = 0


class Prog:
    ENG = ("pe", "act", "dve", "pool", "sp")

    def __init__(self, nc, es):
        self.nc = nc
        self.es = es
        self.q = {e: [] for e in self.ENG}
        self.sem = {e: es.enter_context(nc.semaphore("s_" + e)) for e in self.ENG}
        self.cnt = {e: 0 for e in self.ENG}
        self.known = {e: {} for e in self.ENG}
        self.nbuf = 0
        self.all_sems = [self.sem[e] for e in self.ENG]

    def sb(self, name, shape, dtype):
        return self.es.enter_context(self.nc.sbuf_tensor("sb_" + name, shape, dtype))

    def ps(self, name, shape, dtype):
        return self.es.enter_context(self.nc.psum_tensor("ps_" + name, shape, dtype))

    def buf(self, name=None):
        self.nbuf += 1
        return Buf(name or ("b%d" % self.nbuf))

    def _waits(self, eng, reads, writes, own_skip=()):
        need = {}

        def add(d):
            if d is None:
                return
            s, v = d
            k = id(s)
            if k not in need or need[k][1] < v:
                need[k] = (s, v)

        for b in reads:
            add(b.w)
        for b in writes:
            if b in own_skip and b.w is not None and b.w[0] is self.sem[eng]:
                continue
            add(b.w)
            for d in b.r.values():
                add(d)
        out = []
        kn = self.known[eng]
        for k, (s, v) in need.items():
            if eng == "pe" and s is self.sem["pe"]:
                continue
            if kn.get(k, 0) >= v:
                continue
            kn[k] = v
            out.append((s, v))
        return out

    def op(self, eng, fn, reads=(), writes=(), track=None):
        own_skip = ()
        if any(b.excl for b in reads):
            own_skip = tuple(b for b in reads if b.excl)
            writes = tuple(writes) + own_skip
            reads = tuple(b for b in reads if not b.excl)
        waits = self._waits(eng, reads, writes, own_skip)
        if track is not None:
            if track.sem is None:
                track.sem = self.es.enter_context(self.nc.semaphore("d_" + track.name))
                self.all_sems.append(track.sem)
            track.cnt += 16
            done = (track.sem, track.cnt)
            inc = 16
        else:
            self.cnt[eng] += 1
            done = (self.sem[eng], self.cnt[eng])
            inc = 1
        k = id(done[0])
        for b in reads:
            if k not in b.r or b.r[k][1] < done[1]:
                b.r[k] = done
        for b in writes:
            b.w = done
            b.r = {}
        self.q[eng].append((waits, fn, done[0], inc))

    def wait_all(self, eng, bufs):
        waits = self._waits(eng, bufs, ())
        self.q[eng].append((waits, None, None, 0))

    def emit(self):
        nc = self.nc

        def mk(name):
            def f(e):
                for waits, fn, sem, inc in self.q[name]:
                    for (s, v) in waits:
                        e.wait_ge(s, v)
                    if fn is not None:
                        fn(e).then_inc(sem, inc)
            return f

        sems = list(self.all_sems)

        def clr(e):
            for s in sems:
                e.sem_clear(s)

        with nc.Block() as b0:
            b0.sync(clr)
        with nc.Block() as block:
            block.tensor(mk("pe"))
            block.scalar(mk("act"))
            block.vector(mk("dve"))
            block.gpsimd(mk("pool"))
            block.sync(mk("sp"))


def dma(P, q, out_ap, in_ap, reads, writes, track):
    P.op(q, lambda e: e.dma_start(out=out_ap, in_=in_ap), reads, writes, track=track)


def mm(P, out_ap, lhsT, rhs, start, stop, reads, writes):
    P.op("pe", lambda e: e.matmul(out_ap, lhsT, rhs, start=start, stop=stop), reads, writes)


def ACT(P, out, in_, func, reads, writes, scale=None, bias=None, accum=None):
    kw = {}
    if scale is not None:
        kw["scale"] = scale
    if bias is not None:
        kw["bias"] = bias
    if accum is not None:
        kw["accum_out"] = accum
    P.op("act", lambda e: e.activation(out, in_, func, **kw), reads, writes)


def TS(P, eng, out, in0, s1, s2, op0, op1, reads, writes):
    if op1 is None:
        P.op(eng, lambda e: e.tensor_scalar(out, in0, s1, s2, op0), reads, writes)
    else:
        P.op(eng, lambda e: e.tensor_scalar(out, in0, s1, s2, op0, op1), reads, writes)


def TT(P, eng, out, in0, in1, op, reads, writes):
    P.op(eng, lambda e: e.tensor_tensor(out, in0, in1, op), reads, writes)


def STT(P, eng, out, in0, sc, in1, op0, op1, reads, writes):
    P.op(eng, lambda e: e.scalar_tensor_tensor(out, in0, sc, in1, op0, op1), reads, writes)


def CP(P, eng, out, in_, reads, writes):
    if eng == "act":
        P.op(eng, lambda e: e.copy(out, in_), reads, writes)
    else:
        P.op(eng, lambda e: e.tensor_copy(out, in_), reads, writes)


def TR(P, out, in_, ident, reads, writes):
    P.op("pe", lambda e: e.transpose(out, in_, ident), reads, writes)


class Consts:
    pass


def make_consts(P):
    C = Consts()
    C.ones_f = P.sb("ones_f", [128, 128], F32)
    C.ident_f = P.sb("ident_f", [128, 128], F32)
    C.b_ones = P.buf("ones")
    C.b_ident = P.buf("ident")
    P.op("pool", lambda e: e.memset(C.ones_f[:], 1.0), (), (C.b_ones,))
    P.op("pool", lambda e: e.affine_select(C.ident_f[:], C.ones_f[:], [[-1, 128]], ALU.is_equal, 0.0,
                                           base=0, channel_multiplier=1), (C.b_ones,), (C.b_ident,))
    return C


class TState:
    pass


def alloc_T(P):
    S = TState()
    NTT = NT // 128
    S.H = [P.sb("H%d" % t, [128, D], F32) for t in range(NTT)]
    S.bH = [P.buf("H%d" % t) for t in range(NTT)]
    S.U = [P.sb("U%d" % j, [128, NT], BF16) for j in range(22)]
    S.bU = [P.buf("U%d" % j) for j in range(22)]
    S.hnT = P.sb("hnT", [128, 16, NT], BF16)
    S.bhnT = [P.buf("hnT%d" % t) for t in range(NTT)]
    S.wgu = [[P.sb("wgu%d_%d" % (i, m), [128, 16, 256], BF16) for m in range(2)] for i in range(2)]
    S.bwgu = [[P.buf("wgu%d_%d" % (i, m)) for m in range(2)] for i in range(2)]
    S.NRING = 4
    S.wr = [P.sb("wr%d" % i, [128, 512], BF16) for i in range(S.NRING)]
    S.bwr = [P.buf("wr%d" % i) for i in range(S.NRING)]
    S.ring_i = 0
    S.XS = P.sb("XS", [128, 1024], F32)
    S.bXS = P.buf("XS")
    S.junk = P.sb("junk", [128, D], BF16)
    S.bjunk = P.buf("junk")
    S.sg = [P.sb("sg%d" % i, [128, 512], BF16) for i in range(2)]
    S.bsg = [P.buf("sg%d" % i) for i in range(2)]
    S.ss = P.sb("ss", [128, 8], F32)
    S.bss = [P.buf("ss%d" % t) for t in range(8)]
    S.rstd = P.sb("rstd", [128, 8], F32)
    S.brstd = [P.buf("rstd%d" % t) for t in range(8)]
    S.g = P.sb("gvec", [128, 16], F32)
    S.bg = P.buf("gvec")
    S.PS = [P.ps("PS%d" % i, [128, 512], F32) for i in range(8)]
    S.bPS = [P.buf("PS%d" % i) for i in range(8)]
    return S


def rms_stats(P, S, t):
    ACT(P, S.junk[:], S.H[t][:], AF.Square, (S.bH[t],), (S.bjunk, S.bss[t]), accum=S.ss[:, t:t + 1])
    TS(P, "dve", S.rstd[:, t:t + 1], S.ss[:, t:t + 1], 1.0 / D, EPS, ALU.mult, ALU.add, (S.bss[t],), (S.brstd[t],))
    ACT(P, S.rstd[:, t:t + 1], S.rstd[:, t:t + 1], AF.Sqrt, (S.brstd[t],), (S.brstd[t],))
    P.op("dve", lambda e: e.reciprocal(S.rstd[:, t:t + 1], S.rstd[:, t:t + 1]), (S.brstd[t],), (S.brstd[t],))


def norm_to_featmajor(P, S, C, g_dram, bg_dram):
    NTT = NT // 128
    dma(P, "sp", S.g[:], g_dram, (bg_dram,), (S.bg,), S.bg)
    for t in range(NTT):
        rms_stats(P, S, t)
        for hh in range(2):
            ACT(P, S.XS[:], S.H[t][:, hh * 1024:(hh + 1) * 1024], AF.Copy, (S.bH[t], S.brstd[t]), (S.bXS,),
                scale=S.rstd[:, t:t + 1])
            for q4 in range(2):
                pb = (t * 4 + hh * 2 + q4) % 8
                for i in range(4):
                    kk = q4 * 4 + i
                    TR(P, S.PS[pb][:, i * 128:(i + 1) * 128], S.XS[:, kk * 128:(kk + 1) * 128], C.ident_f[:],
                       (S.bXS, C.b_ident), (S.bPS[pb],))
                for i in range(4):
                    k = hh * 8 + q4 * 4 + i
                    TS(P, "dve", S.hnT[:, k, t * 128:(t + 1) * 128], S.PS[pb][:, i * 128:(i + 1) * 128],
                       S.g[:, k:k + 1], None, ALU.mult, None, (S.bPS[pb], S.bg), (S.bhnT[t],))


def ring_load(P, S, src_ap, bsrc):
    i = S.ring_i % S.NRING
    S.ring_i += 1
    dma(P, "pool", S.wr[i][:], src_ap, (bsrc,), (S.bwr[i],), S.bwr[i])
    return i


def proj_accumulate(P, S, nk, w_dram, bw, krow0):
    NTT = NT // 128
    for n in range(D // 512):
        for k in range(nk):
            i = ring_load(P, S, w_dram[krow0 + k * 128: krow0 + (k + 1) * 128, n * 512:(n + 1) * 512], bw)
            for t in range(NTT):
                mm(P, S.PS[t][:], S.U[k][:, t * 128:(t + 1) * 128], S.wr[i][:], k == 0, k == nk - 1,
                   (S.bU[k], S.bwr[i]), (S.bPS[t],))
        for t in range(NTT):
            TT(P, "dve", S.H[t][:, n * 512:(n + 1) * 512], S.PS[t][:], S.H[t][:, n * 512:(n + 1) * 512], ALU.add,
               (S.bPS[t], S.bH[t]), (S.bH[t],))


def phase_T(P, S, C, dr, DV, final):
    NTT = NT // 128
    nkc = DV // 128
    for kb in range(0, nkc, 16):
        nk = min(16, nkc - kb)
        for k in range(nk):
            dma(P, "sp", S.U[k][:], dr["oT"][(kb + k) * 128:(kb + k + 1) * 128, :], (dr["b_oT"],), (S.bU[k],), S.bU[k])
        proj_accumulate(P, S, nk, dr["w_out"], dr["b_w"], kb * 128)
    stage = dr.get("stage", 99)
    if stage <= 1:
        for t in range(NTT):
            dma(P, "sp", dr["h_out"][t * 128:(t + 1) * 128, :], S.H[t][:], (S.bH[t],), (dr["b_hout"],), dr["b_hout"])
        P.wait_all("sp", [dr["b_hout"]])
        return
    norm_to_featmajor(P, S, C, dr["g_ffn"], dr["b_w"])
    if stage <= 2:
        for k in range(16):
            dma(P, "sp", dr["hnT_out"][k * 128:(k + 1) * 128, :], S.hnT[:, k, :], tuple(S.bhnT), (dr["b_hnout"],),
                dr["b_hnout"])
        P.wait_all("sp", [dr["b_hnout"]])
        return
    wg_v = dr["w_gate"].rearrange("(k p) c -> p k c", p=128)
    wu_v = dr["w_up"].rearrange("(k p) c -> p k c", p=128)
    grp = 0
    for hf in range(2):
        for jj in range(11):
            c0 = (hf * 22 + jj * 2) * 128
            sl = grp % 2
            grp += 1
            dma(P, "pool", S.wgu[sl][0][:], wg_v[:, :, c0:c0 + 256], (dr["b_w"],), (S.bwgu[sl][0],), S.bwgu[sl][0])
            dma(P, "pool", S.wgu[sl][1][:], wu_v[:, :, c0:c0 + 256], (dr["b_w"],), (S.bwgu[sl][1],), S.bwgu[sl][1])
            for j in range(2):
                jc = jj * 2 + j
                pset = (jc % 2) * 4
                for k in range(16):
                    for m in range(2):
                        for t2 in range(2):
                            pb = pset + m * 2 + t2
                            mm(P, S.PS[pb][:], S.wgu[sl][m][:, k, j * 128:(j + 1) * 128],
                               S.hnT[:, k, t2 * 512:(t2 + 1) * 512], k == 0, k == 15,
                               (S.bwgu[sl][m],) + tuple(S.bhnT[t2 * 4:(t2 + 1) * 4]), (S.bPS[pb],))
                for t2 in range(2):
                    pg = pset + t2
                    pu = pset + 2 + t2
                    ACT(P, S.sg[t2][:], S.PS[pg][:], AF.Silu, (S.bPS[pg],), (S.bsg[t2],))
                    TT(P, "dve", S.U[jc][:, t2 * 512:(t2 + 1) * 512], S.PS[pu][:], S.sg[t2][:], ALU.mult,
                       (S.bsg[t2], S.bPS[pu]), (S.bU[jc],))
        if stage == 3:
            for k in range(16):
                dma(P, "sp", dr["hnT_out"][k * 128:(k + 1) * 128, :], S.U[k][:], (S.bU[k],), (dr["b_hnout"],),
                    dr["b_hnout"])
            P.wait_all("sp", [dr["b_hnout"]])
            return
        proj_accumulate(P, S, 22, dr["w_down"], dr["b_w"], hf * 22 * 128)
        if stage == 4:
            for t in range(NTT):
                dma(P, "sp", dr["h_out"][t * 128:(t + 1) * 128, :], S.H[t][:], (S.bH[t],), (dr["b_hout"],), dr["b_hout"])
            P.wait_all("sp", [dr["b_hout"]])
            return
    if not final:
        for t in range(NTT):
            dma(P, "sp", dr["h_out"][t * 128:(t + 1) * 128, :], S.H[t][:], (S.bH[t],), (dr["b_hout"],), dr["b_hout"])
        norm_to_featmajor(P, S, C, dr["g_next"], dr["b_w"])
        for k in range(16):
            dma(P, "sp", dr["hnT_out"][k * 128:(k + 1) * 128, :], S.hnT[:, k, :], tuple(S.bhnT), (dr["b_hnout"],),
                dr["b_hnout"])
        P.wait_all("sp", [dr["b_hout"], dr["b_hnout"]])
    else:
        for t in range(NTT):
            rms_stats(P, S, t)
        for hh in range(2):
            dma(P, "sp", S.XS[:], dr["g_final_b"][:, hh * 1024:(hh + 1) * 1024], (dr["b_w"],), (S.bXS,), S.bXS)
            for t in range(NTT):
                STT(P, "dve", S.H[t][:, hh * 1024:(hh + 1) * 1024], S.H[t][:, hh * 1024:(hh + 1) * 1024],
                    S.rstd[:, t:t + 1], S.XS[:], ALU.mult, ALU.mult, (S.bH[t], S.brstd[t], S.bXS), (S.bH[t],))
        for t in range(NTT):
            dma(P, "sp", dr["out"][t * 128:(t + 1) * 128, :], S.H[t][:], (S.bH[t],), (dr["b_out"],), dr["b_out"])
        P.wait_all("sp", [dr["b_out"]])


def build_T(DV, final, stage=99):
    nc = bass.Bass("TRN2", target_bir_lowering=False)
    es = ExitStack()
    dr = {}
    dr["h_in"] = nc.dram_tensor("h_in", [NT, D], F32, kind="ExternalInput").ap()
    dr["oT"] = nc.dram_tensor("oT", [DV, NT], BF16, kind="ExternalInput").ap()
    dr["w_out"] = nc.dram_tensor("w_out", [DV, D], F32, kind="ExternalInput").ap()
    dr["g_ffn"] = nc.dram_tensor("g_ffn", [128, 16], F32, kind="ExternalInput").ap()
    dr["w_gate"] = nc.dram_tensor("w_gate", [D, DFF], F32, kind="ExternalInput").ap()
    dr["w_up"] = nc.dram_tensor("w_up", [D, DFF], F32, kind="ExternalInput").ap()
    dr["w_down"] = nc.dram_tensor("w_down", [DFF, D], F32, kind="ExternalInput").ap()
    if final:
        dr["g_final_b"] = nc.dram_tensor("g_final_b", [128, D], F32, kind="ExternalInput").ap()
        dr["out"] = nc.dram_tensor("out", [NT, D], F32, kind="ExternalOutput").ap()
    else:
        dr["g_next"] = nc.dram_tensor("g_next", [128, 16], F32, kind="ExternalInput").ap()
        dr["h_out"] = nc.dram_tensor("h_out", [NT, D], F32, kind="ExternalOutput").ap()
        dr["hnT_out"] = nc.dram_tensor("hnT_out", [D, NT], BF16, kind="ExternalOutput").ap()
    dr["stage"] = stage
    with es:
        P = Prog(nc, es)
        for nm in ("b_oT", "b_w", "b_hin", "b_hout", "b_hnout", "b_out"):
            dr[nm] = P.buf(nm)
        C = make_consts(P)
        S = alloc_T(P)
        for t in range(NT // 128):
            dma(P, "sp", S.H[t][:], dr["h_in"][t * 128:(t + 1) * 128, :], (dr["b_hin"],), (S.bH[t],), S.bH[t])
        phase_T(P, S, C, dr, DV, final)
        P.emit()
    return nc


def build_N():
    nc = bass.Bass("TRN2", target_bir_lowering=False)
    es = ExitStack()
    dr = {}
    dr["h_in"] = nc.dram_tensor("h_in", [NT, D], F32, kind="ExternalInput").ap()
    dr["g_next"] = nc.dram_tensor("g_next", [128, 16], F32, kind="ExternalInput").ap()
    dr["hnT_out"] = nc.dram_tensor("hnT_out", [D, NT], BF16, kind="ExternalOutput").ap()
    with es:
        P = Prog(nc, es)
        for nm in ("b_w", "b_hin", "b_hnout"):
            dr[nm] = P.buf(nm)
        C = make_consts(P)
        S = alloc_T(P)
        for t in range(NT // 128):
            dma(P, "sp", S.H[t][:], dr["h_in"][t * 128:(t + 1) * 128, :], (dr["b_hin"],), (S.bH[t],), S.bH[t])
        norm_to_featmajor(P, S, C, dr["g_next"], dr["b_w"])
        for k in range(16):
            dma(P, "sp", dr["hnT_out"][k * 128:(k + 1) * 128, :], S.hnT[:, k, :], tuple(S.bhnT), (dr["b_hnout"],),
                dr["b_hnout"])
        P.wait_all("sp", [dr["b_hnout"]])
        P.emit()
    return nc


class Pool2:
    def __init__(self, P, name, n, shape, dtype, psum=False, banks=None):
        self.n = n
        self.i = 0
        if psum:
            self.t = []
            self.b = []
            per = 512 // shape[1]
            for bk in banks:
                for s in range(per):
                    self.t.append(bk[:shape[0], s * shape[1]:(s + 1) * shape[1]])
                    self.b.append(P.buf("%s%d" % (name, len(self.b))))
            self.n = len(self.t)
        else:
            T = P.sb(name, [shape[0], n, shape[1]], dtype)
            self.t = [T[:, i, :] for i in range(n)]
            self.b = [P.buf("%s%d" % (name, i)) for i in range(n)]

    def get(self):
        i = self.i % self.n
        self.i += 1
        return self.t[i], self.b[i]


def rsqrt_ops(P, out, in_, addc, mulc, rd, wr):
    TS(P, "dve", out, in_, mulc, addc, ALU.mult, ALU.add, rd, wr)
    ACT(P, out, out, AF.Sqrt, wr, wr)
    P.op("dve", lambda e: e.reciprocal(out, out), wr, wr)


class PsumPool:
    def __init__(self, P, name, banks):
        self.banks = banks
        self.bufs = [P.buf("%s%d" % (name, i)) for i in range(len(banks))]
        for b in self.bufs:
            b.excl = True
        self.i = 0

    def align(self):
        self.i = (self.i + 3) // 4 * 4

    def get(self):
        b = (self.i // 4) % len(self.banks)
        s_ = self.i % 4
        self.i += 1
        return self.banks[b][:, s_ * 128:(s_ + 1) * 128], self.bufs[b]


def groups4(seq):
    seq = list(seq)
    return [seq[i:i + 4] for i in range(0, len(seq), 4)]


def emit_out_norm4(P, C, outs, gain_ap, bgain, f128, sm_p, psC):
    psC.align()
    slots = []
    for (o, bo, szT_ap, bsz, oT_ap, boT) in outs:
        sm, bsm = sm_p.get()
        junk, bjunk = f128.get()
        ACT(P, junk, o, AF.Square, (bo,), (bjunk, bsm), accum=sm[:, 0:1])
        rsqrt_ops(P, sm[:, 1:2], sm[:, 0:1], 1e-6, 1.0 / 128, (bsm,), (bsm,))
        STT(P, "dve", o, o, sm[:, 1:2], gain_ap, ALU.mult, ALU.mult, (bo, bsm, bgain), (bo,))
    for (o, bo, szT_ap, bsz, oT_ap, boT) in outs:
        p5, b5 = psC.get()
        TR(P, p5[:, 0:64], o, C.ident_f[:64, :64], (bo, C.b_ident), (b5,))
        slots.append((p5, b5))
    for (o, bo, szT_ap, bsz, oT_ap, boT), (p5, b5) in zip(outs, slots):
        TT(P, "dve", oT_ap, p5[:, 0:64], szT_ap, ALU.mult, (b5, bsz), (boT,))
    psC.align()


def mixer_common(P):
    C = make_consts(P)
    C.ident_b = P.sb("ident_b", [128, 128], BF16)
    C.b_identb = P.buf("identb")
    CP(P, "dve", C.ident_b[:], C.ident_f[:], (C.b_ident,), (C.b_identb,))
    C.tri = P.sb("tri", [64, 64], F32)
    C.b_tri = P.buf("tri")
    C.mst = P.sb("mst", [64, 64], F32)
    C.b_mst = P.buf("mst")
    P.op("pool", lambda e: e.affine_select(C.tri[:], C.ones_f[:64, :64], [[1, 64]], ALU.is_ge, 0.0, base=0,
                                           channel_multiplier=-1), (C.b_ones,), (C.b_tri,))
    P.op("pool", lambda e: e.affine_select(C.mst[:], C.ones_f[:64, :64], [[-1, 64]], ALU.is_gt, 0.0, base=0,
                                           channel_multiplier=1), (C.b_ones,), (C.b_mst,))
    return C


def build_G(S_len=SEQ, dbg=99):
    nc = bass.Bass("TRN2", target_bir_lowering=False)
    es = ExitStack()
    hnT_d = nc.dram_tensor("hnT", [D, S_len], BF16, kind="ExternalInput").ap()
    w_d = nc.dram_tensor("w", [D, 1536], F32, kind="ExternalInput").ap()
    wba_d = nc.dram_tensor("wba", [D, 8], F32, kind="ExternalInput").ap()
    cw_d = nc.dram_tensor("cw", [128, 32], F32, kind="ExternalInput").ap()
    par_d = nc.dram_tensor("par", [128, 192], F32, kind="ExternalInput").ap()
    oT_d = nc.dram_tensor("oT", [512, S_len], BF16, kind="ExternalOutput").ap()
    NTILE = S_len // 512
    with es:
        P = Prog(nc, es)
        bin_ = P.buf("in")
        bout = P.buf("out")
        C = mixer_common(P)
        tri, b_tri, mst, b_mst = C.tri, C.b_tri, C.mst, C.b_mst
        W = P.sb("W", [128, 16, 1536], BF16); bW = P.buf("W")
        wv = w_d.rearrange("(k p) c -> p k c", p=128)
        for kq in range(4):
            dma(P, "pool", W[:, kq * 4:(kq + 1) * 4, :], wv[:, kq * 4:(kq + 1) * 4, :], (bin_,), (bW,), bW)
        Wba = P.sb("Wba", [128, 16, 8], BF16); bWba = P.buf("Wba")
        dma(P, "pool", Wba[:], wba_d.rearrange("(k p) c -> p k c", p=128), (bin_,), (bWba,), bWba)
        cw = P.sb("cw", [128, 32], F32); bcw = P.buf("cw")
        dma(P, "sp", cw[:], cw_d, (bin_,), (bcw,), bcw)
        par = P.sb("par", [128, 192], F32); bpar = P.buf("par")
        dma(P, "sp", par[:], par_d, (bin_,), (bpar,), bpar)
        negA = P.sb("negA", [128, 32], F32); bnegA = P.buf("negA")
        ACT(P, negA[:], par[:, 0:32], AF.Exp, (bpar,), (bnegA,))
        TS(P, "dve", negA[:], negA[:], -1.0, None, ALU.mult, None, (bnegA,), (bnegA,))
        Sf = [P.sb("Sf%d" % h, [128, 128], F32) for h in range(4)]
        bSf = [P.buf("Sf%d" % h) for h in range(4)]
        Sb = [[P.sb("Sb%d_%d" % (h, i), [128, 128], BF16) for i in range(2)] for h in range(4)]
        bSb = [[P.buf("Sb%d_%d" % (h, i)) for i in range(2)] for h in range(4)]
        for h in range(4):
            P.op("pool", lambda e, h=h: e.memset(Sf[h][:], 0.0), (), (bSf[h],))
            P.op("pool", lambda e, h=h: e.memset(Sb[h][0][:], 0.0), (), (bSb[h][0],))
        X = P.sb("X", [128, 16, 512], BF16); bX = P.buf("X")
        pre = [P.sb("pre%d" % ct, [128, 515], F32) for ct in range(8)]
        bpre = [P.buf("pre%d" % ct) for ct in range(8)]
        for ct in range(8):
            P.op("pool", lambda e, ct=ct: e.memset(pre[ct][:, 0:3], 0.0), (), (bpre[ct],))
        acc = [P.sb("acc%d" % ct, [128, 512], F32) for ct in range(8)]
        bacc = [P.buf("acc%d" % ct) for ct in range(8)]
        a16 = [P.sb("a16_%d" % ct, [128, 512], BF16) for ct in range(8)]
        ba16 = [P.buf("a16_%d" % ct) for ct in range(8)]
        sz = [P.sb("sz%d" % h, [128, 512], BF16) for h in range(4)]
        bsz = [P.buf("sz%d" % h) for h in range(4)]
        tmpA = P.sb("tmpA", [128, 512], F32); btmpA = P.buf("tmpA")
        tmpB = P.sb("tmpB", [128, 512], F32); btmpB = P.buf("tmpB")
        ktok = P.sb("ktok", [64, 16, 128], BF16)
        bktok = [P.buf("ktok%d" % i) for i in range(16)]
        vtok = P.sb("vtok", [64, 32, 128], BF16)
        bvtok = [P.buf("vtok%d" % i) for i in range(32)]
        GN = ("beta", "nbeta", "g", "G", "expG", "kdsc", "bexpG")
        gt = {nm: P.sb("gt_" + nm, [64, 32], F32) for nm in GN}
        gl = P.sb("gt_gl", [128, 32], F32)
        bgate = P.buf("gates")
        oTt = [P.sb("oTt%d" % h, [128, 512], BF16) for h in range(4)]
        boTt = [P.buf("oTt%d" % h) for h in range(4)]
        PSb = [P.ps("PSb%d" % i, [128, 512], F32) for i in range(8)]
        bGA = [P.buf("GA0"), P.buf("GA1")]
        bF = P.buf("F2")
        for b_ in bGA + [bF]:
            b_.excl = True
        FB = PSb[2]
        psA = PsumPool(P, "psAB", [PSb[3], PSb[4]])
        psB = psA
        psC = PsumPool(P, "psC", [PSb[5], PSb[6], PSb[7]])
        NI = 16
        m64 = {nm: Pool2(P, nm, NI, [64, 64], BF16) for nm in ("Pa", "Pb", "PTa", "PTb", "TTa", "TTb")}
        m64["Em"] = Pool2(P, "Em", 8, [64, 64], F32)
        m64["ETm"] = Pool2(P, "ETm", 8, [64, 64], F32)
        dg_p = Pool2(P, "dg4", 2, [64, 256], F32)
        aq_p = Pool2(P, "aqkT", NI, [64, 64], BF16)
        u_p = Pool2(P, "u", NI, [64, 128], F32)
        wT_p = Pool2(P, "wT", NI, [128, 64], BF16)
        kd_p = Pool2(P, "kd", NI, [64, 128], BF16)
        b128 = Pool2(P, "b128", 8, [64, 128], BF16)
        f128 = Pool2(P, "f128", 8, [64, 128], F32)
        vn_p = Pool2(P, "vn", 8, [64, 128], BF16)
        o_p = Pool2(P, "o", 8, [64, 128], F32)
        ssc_p = Pool2(P, "ssc", 4, [128, 128], F32)
        sm_p = Pool2(P, "sm", 16, [64, 2], F32)

        hv = hnT_d.rearrange("(k p) t -> p k t", p=128)
        for ti in range(NTILE):
            t0 = ti * 512
            dma(P, "sp", X[:], hv[:, :, t0:t0 + 512], (bin_,), (bX,), bX)
            for ct in range(12):
                ga = ct % 2
                for k in range(16):
                    mm(P, PSb[ga][:], W[:, k, ct * 128:(ct + 1) * 128], X[:, k, :], k == 0, k == 15,
                       (bW, bX), (bGA[ga],))
                if ct < 8:
                    CP(P, "act", pre[ct][:, 3:515], PSb[ga][:], (bGA[ga],), (bpre[ct],))
                    TS(P, "dve", acc[ct][:], pre[ct][:, 3:515], cw[:, ct * 4 + 3:ct * 4 + 4], None, ALU.mult, None,
                       (bpre[ct], bcw), (bacc[ct],))
                    for tap in range(3):
                        STT(P, "dve", acc[ct][:], pre[ct][:, tap:tap + 512], cw[:, ct * 4 + tap:ct * 4 + tap + 1],
                            acc[ct][:], ALU.mult, ALU.add, (bpre[ct], bcw, bacc[ct]), (bacc[ct],))
                    CP(P, "pool", pre[ct][:, 0:3], pre[ct][:, 512:515], (bpre[ct],), (bpre[ct],))
                    if ct < 4:
                        ACT(P, acc[ct][:], acc[ct][:], AF.Silu, (bacc[ct],), (bacc[ct],))
                    else:
                        ACT(P, a16[ct][:], acc[ct][:], AF.Silu, (bacc[ct],), (ba16[ct],))
                else:
                    ACT(P, sz[ct - 8][:], PSb[ga][:], AF.Silu, (bGA[ga],), (bsz[ct - 8],))
            for ct in range(4):
                TT(P, "pool", tmpA[:], acc[ct][:], acc[ct][:], ALU.mult, (bacc[ct],), (btmpA,))
                ga = ct % 2
                mm(P, PSb[ga][:], C.ones_f[:], tmpA[:], True, True, (C.b_ones, btmpA), (bGA[ga],))
                rsqrt_ops(P, tmpB[:], PSb[ga][:], 1e-6, 1.0, (bGA[ga],), (btmpB,))
                if ct < 2:
                    STT(P, "dve", a16[ct][:], acc[ct][:], 128.0 ** -0.5, tmpB[:], ALU.mult, ALU.mult,
                        (bacc[ct], btmpB), (ba16[ct],))
                else:
                    TT(P, "dve", a16[ct][:], acc[ct][:], tmpB[:], ALU.mult, (bacc[ct], btmpB), (ba16[ct],))
            if dbg == 0:
                break
            jobs = []
            for c in range(8):
                for hk in range(2):
                    jobs.append((2 + hk, c, ktok[:, hk * 8 + c, :], bktok[hk * 8 + c]))
                for h in range(4):
                    jobs.append((4 + h, c, vtok[:, h * 8 + c, :], bvtok[h * 8 + c]))
            for gi, grp in enumerate(groups4(jobs)):
                psA.align()
                sl = []
                for (ct, c, dst, bdst) in grp:
                    pt, pbf = psA.get()
                    mm(P, pt[:64, :], a16[ct][:, c * 64:(c + 1) * 64], C.ident_b[:], True, True,
                       (ba16[ct], C.b_identb), (pbf,))
                    sl.append((pt, pbf))
                for (ct, c, dst, bdst), (pt, pbf) in zip(grp, sl):
                    CP(P, "act" if gi % 2 == 0 else "dve", dst, pt[:64, :], (pbf,), (bdst,))
            psA.align()
            if dbg == 1:
                break
            psA.align()
            pt, pbf = psA.get()
            psA.align()
            for c in range(8):
                for k in range(16):
                    mm(P, pt[:64, c * 8:(c + 1) * 8], X[:, k, c * 64:(c + 1) * 64], Wba[:, k, :], k == 0, k == 15,
                       (bX, bWba), (pbf,))
            ba3 = pt[:64, 0:64].rearrange("p (c x) -> p c x", x=8)
            v3 = lambda ap: ap.rearrange("p (c h) -> p c h", h=4)
            ACT(P, v3(gt["beta"][:]), ba3[:, :, 0:4], AF.Sigmoid, (pbf,), (bgate,))
            TT(P, "dve", v3(gt["g"][:]), ba3[:, :, 4:8], v3(par[:64, 32:64]), ALU.add, (pbf, bpar), (bgate,))
            ACT(P, gt["g"][:], gt["g"][:], AF.Exp, (bgate,), (bgate,))
            TS(P, "dve", gt["g"][:], gt["g"][:], 1.0, None, ALU.add, None, (bgate,), (bgate,))
            ACT(P, gt["g"][:], gt["g"][:], AF.Ln, (bgate,), (bgate,))
            TT(P, "dve", gt["g"][:], gt["g"][:], negA[:64, :], ALU.mult, (bgate, bnegA), (bgate,))
            TS(P, "dve", gt["nbeta"][:], gt["beta"][:], -1.0, None, ALU.mult, None, (bgate,), (bgate,))
            mm(P, FB[:64, 0:32], tri[:], gt["g"][:], True, True, (b_tri, bgate), (bF,))
            mm(P, FB[:, 32:64], C.ones_f[:64, :], gt["g"][:], True, True, (C.b_ones, bgate), (bF,))
            CP(P, "dve", gt["G"][:], FB[:64, 0:32], (bF,), (bgate,))
            ACT(P, gt["expG"][:], FB[:64, 0:32], AF.Exp, (bF,), (bgate,))
            ACT(P, gl[:], FB[:, 32:64], AF.Exp, (bF,), (bgate,))
            TT(P, "dve", gt["kdsc"][:], FB[:64, 32:64], gt["G"][:], ALU.subtract, (bF, bgate), (bgate,))
            ACT(P, gt["kdsc"][:], gt["kdsc"][:], AF.Exp, (bgate,), (bgate,))
            TT(P, "dve", gt["bexpG"][:], gt["beta"][:], gt["expG"][:], ALU.mult, (bgate,), (bgate,))
            if dbg == 2:
                break

            item = {}

            def rec_chunk(c, ti=ti, item=item):
                par_i = (ti * 8 + c) % 2
                R = {h: {} for h in range(4)}
                psC.align()
                for h in range(4):
                    s_ = item[(h, c)]
                    p1, b1 = psC.get()
                    mm(P, p1[:64, :], s_["wT"], Sb[h][par_i][:], True, True, (s_["bwT"], bSb[h][par_i]), (b1,))
                    R[h].update(ws=p1, bws=b1)
                for h in range(4):
                    p2, b2 = psC.get()
                    mm(P, p2[:64, :], a16[h // 2][:, c * 64:(c + 1) * 64], Sb[h][par_i][:], True, True,
                       (ba16[h // 2], bSb[h][par_i]), (b2,))
                    R[h].update(qs=p2, bqs=b2)
                for h in range(4):
                    s_ = item[(h, c)]
                    vn, bvn = vn_p.get()
                    STT(P, "dve", vn, R[h]["ws"][:64, :], -1.0, s_["u"], ALU.mult, ALU.add, (R[h]["bws"], s_["bu"]), (bvn,))
                    R[h].update(vn=vn, bvn=bvn)
                for h in range(4):
                    s_ = item[(h, c)]
                    p3, b3 = psC.get()
                    mm(P, p3[:64, :], s_["aq"], R[h]["vn"], True, True, (s_["baq"], R[h]["bvn"]), (b3,))
                    R[h].update(av=p3, bav=b3)
                for h in range(4):
                    s_ = item[(h, c)]
                    p4, b4 = psC.get()
                    mm(P, p4[:, :], s_["kd"], R[h]["vn"], True, True, (s_["bkd"], R[h]["bvn"]), (b4,))
                    R[h].update(kv=p4, bkv=b4)
                for h in range(4):
                    ci = c * 4 + h
                    avs, bavs = f128.get()
                    CP(P, "act", avs, R[h]["av"][:64, :], (R[h]["bav"],), (bavs,))
                    o, bo = o_p.get()
                    STT(P, "dve", o, R[h]["qs"][:64, :], gt["expG"][:, ci:ci + 1], avs, ALU.mult, ALU.add,
                        (R[h]["bqs"], bgate, bavs), (bo,))
                    R[h].update(o=o, bo=bo)
                for h in range(4):
                    ci = c * 4 + h
                    ssc, bssc = ssc_p.get()
                    ACT(P, ssc, Sf[h][:], AF.Copy, (bSf[h], bgate), (bssc,), scale=gl[:, ci:ci + 1])
                    TT(P, "dve", Sf[h][:], R[h]["kv"][:, :], ssc, ALU.add, (R[h]["bkv"], bssc), (bSf[h],))
                    CP(P, "act", Sb[h][1 - par_i][:], Sf[h][:], (bSf[h],), (bSb[h][1 - par_i],))
                emit_out_norm4(P, C, [(R[h]["o"], R[h]["bo"], sz[h][:, c * 64:(c + 1) * 64], bsz[h],
                                       oTt[h][:, c * 64:(c + 1) * 64], boTt[h]) for h in range(4)],
                               par[:64, 64:192], bpar, f128, sm_p, psC)

            for half in range(2):
                items = [(h, c) for c in range(half * 4, half * 4 + 4) for h in range(4)]
                st = {}
                for cp in range(2):
                    cs = [half * 4 + cp * 2, half * 4 + cp * 2 + 1]
                    ems = {}
                    for c in cs:
                        dg, bdg = dg_p.get()
                        for h in range(4):
                            ci = c * 4 + h
                            TS(P, "dve", dg[:, h * 64:(h + 1) * 64], C.ident_f[:64, :64], gt["G"][:, ci:ci + 1], None,
                               ALU.mult, None, (C.b_ident, bgate), (bdg,))
                        mm(P, FB[:64, 0:256], C.ones_f[:64, :64], dg, True, True, (C.b_ones, bdg), (bF,))
                        for h in range(4):
                            ci = c * 4 + h
                            gcol = gt["G"][:, ci:ci + 1]
                            Em, bEm = m64["Em"].get()
                            ETm, bETm = m64["ETm"].get()
                            TS(P, "dve", Em, FB[:64, h * 64:(h + 1) * 64], gcol, 0.0, ALU.subtract, ALU.max, (bF, bgate), (bEm,))
                            TS(P, "dve", ETm, FB[:64, h * 64:(h + 1) * 64], gcol, 0.0, ALU.subtract, ALU.min, (bF, bgate), (bETm,))
                            ACT(P, Em, Em, AF.Exp, (bEm,), (bEm,), scale=-1.0)
                            ACT(P, ETm, ETm, AF.Exp, (bETm,), (bETm,))
                            TT(P, "pool", Em, Em, mst[:], ALU.mult, (bEm, b_mst), (bEm,))
                            TT(P, "pool", ETm, ETm, tri[:], ALU.mult, (bETm, b_tri), (bETm,))
                            ems[(h, c)] = (Em, bEm, ETm, bETm)
                    psB.align()
                    kk = {}
                    for c in cs:
                        for hk in range(2):
                            kT = a16[2 + hk][:, c * 64:(c + 1) * 64]
                            qT = a16[hk][:, c * 64:(c + 1) * 64]
                            pt, pbf = psB.get()
                            mm(P, pt[:64, 0:64], kT, kT, True, True, (ba16[2 + hk],), (pbf,))
                            mm(P, pt[:64, 64:128], kT, qT, True, True, (ba16[2 + hk], ba16[hk]), (pbf,))
                            kk[(hk, c)] = (pt, pbf)
                    for c in cs:
                        for h in range(4):
                            ci = c * 4 + h
                            Em, bEm, ETm, bETm = ems[(h, c)]
                            pt, pbf = kk[(h // 2, c)]
                            Pa, bPa = m64["Pa"].get()
                            STT(P, "dve", Pa, pt[:64, 0:64], gt["nbeta"][:, ci:ci + 1], Em, ALU.mult, ALU.mult,
                                (pbf, bgate, bEm), (bPa,))
                            aq, baq = aq_p.get()
                            TT(P, "dve", aq, pt[:64, 64:128], ETm, ALU.mult, (pbf, bETm), (baq,))
                            st[(h, c)] = dict(P=Pa, bP=bPa, aq=aq, baq=baq)
                if dbg == 3:
                    break
                for grp in groups4(items):
                    psB.align()
                    sl = {}
                    for it in grp:
                        s_ = st[it]
                        pt, pbf = psB.get()
                        mm(P, pt[:64, 0:64], s_["P"], C.ident_b[:64, :64], True, True, (s_["bP"], C.b_identb), (pbf,))
                        sl[it] = (pt, pbf)
                    for it in grp:
                        s_ = st[it]
                        pt, pbf = sl[it]
                        PT, bPT = m64["PTa"].get()
                        CP(P, "act", PT, pt[:64, 0:64], (pbf,), (bPT,))
                        s_.update(PT=PT, bPT=bPT)
                    for it in grp:
                        s_ = st[it]
                        pt, pbf = sl[it]
                        TTm, bTT = m64["TTa"].get()
                        TT(P, "dve", TTm, pt[:64, 0:64], C.ident_f[:64, :64], ALU.add, (pbf, C.b_ident), (bTT,))
                        s_.update(TT=TTm, bTT=bTT)
                for lvl in range(5):
                    nP = "Pb" if lvl % 2 == 0 else "Pa"
                    nPT = "PTb" if lvl % 2 == 0 else "PTa"
                    nTT = "TTb" if lvl % 2 == 0 else "TTa"
                    for grp in groups4(items):
                        psB.align()
                        sl = {}
                        for it in grp:
                            s_ = st[it]
                            pt, pbf = psB.get()
                            mm(P, pt[:64, 0:64], s_["PT"], s_["P"], True, True, (s_["bPT"], s_["bP"]), (pbf,))
                            if lvl < 4:
                                mm(P, pt[:64, 64:128], s_["P"], s_["PT"], True, True, (s_["bPT"], s_["bP"]), (pbf,))
                            sl[it] = (pt, pbf)
                        for it in grp:
                            s_ = st[it]
                            pt, pbf = sl[it]
                            Pn, bPn = m64[nP].get()
                            CP(P, "act", Pn, pt[:64, 0:64], (pbf,), (bPn,))
                            s_.update(Pn_=Pn, bPn_=bPn)
                        if lvl < 4:
                            for it in grp:
                                s_ = st[it]
                                pt, pbf = sl[it]
                                PTn, bPTn = m64[nPT].get()
                                CP(P, "dve", PTn, pt[:64, 64:128], (pbf,), (bPTn,))
                                s_.update(PT=PTn, bPT=bPTn)
                        for it in grp:
                            s_ = st[it]
                            s_.update(P=s_["Pn_"], bP=s_["bPn_"])
                    for grp in groups4(items):
                        psB.align()
                        sl = {}
                        for it in grp:
                            s_ = st[it]
                            pt, pbf = psB.get()
                            mm(P, pt[:64, 0:64], s_["P"], s_["TT"], True, True, (s_["bP"], s_["bTT"]), (pbf,))
                            sl[it] = (pt, pbf)
                        for it in grp:
                            s_ = st[it]
                            pt, pbf = sl[it]
                            TTn, bTTn = m64[nTT].get()
                            TT(P, "dve", TTn, pt[:64, 0:64], s_["TT"], ALU.add, (pbf, s_["bTT"]), (bTTn,))
                            s_.update(TT=TTn, bTT=bTTn)
                if dbg == 4:
                    break
                for grp in [items[i:i + 2] for i in range(0, len(items), 2)]:
                    psB.align()
                    sl = {}
                    for (h, c) in grp:
                        s_ = st[(h, c)]
                        ci = c * 4 + h
                        hk = h // 2
                        vb, bvb = b128.get()
                        TS(P, "pool", vb, vtok[:, h * 8 + c, :], gt["beta"][:, ci:ci + 1], None, ALU.mult, None,
                           (bvtok[h * 8 + c], bgate), (bvb,))
                        kbg, bkbg = b128.get()
                        TS(P, "pool", kbg, ktok[:, hk * 8 + c, :], gt["bexpG"][:, ci:ci + 1], None, ALU.mult, None,
                           (bktok[hk * 8 + c], bgate), (bkbg,))
                        kd, bkd = kd_p.get()
                        TS(P, "pool", kd, ktok[:, hk * 8 + c, :], gt["kdsc"][:, ci:ci + 1], None, ALU.mult, None,
                           (bktok[hk * 8 + c], bgate), (bkd,))
                        pt, pbf = psB.get()
                        mm(P, pt[:64, :], s_["TT"], vb, True, True, (s_["bTT"], bvb), (pbf,))
                        pt2, pbf2 = psB.get()
                        mm(P, pt2[:, 0:64], kbg, s_["TT"], True, True, (s_["bTT"], bkbg), (pbf2,))
                        sl[(h, c)] = (pt, pbf, pt2, pbf2)
                        s_.update(kd=kd, bkd=bkd)
                    for (h, c) in grp:
                        s_ = st[(h, c)]
                        pt, pbf, pt2, pbf2 = sl[(h, c)]
                        u, bu = u_p.get()
                        CP(P, "act", u, pt[:64, :], (pbf,), (bu,))
                        wT, bwT = wT_p.get()
                        CP(P, "dve", wT, pt2[:, 0:64], (pbf2,), (bwT,))
                        s_.update(u=u, bu=bu, wT=wT, bwT=bwT)
                psB.align()
                item.update(st)
                if dbg == 5:
                    break
                for c in range(half * 4, half * 4 + 4):
                    rec_chunk(c)
            if dbg < 99:
                break
            for h in range(4):
                dma(P, "sp", oT_d[h * 128:(h + 1) * 128, t0:t0 + 512], oTt[h][:], (boTt[h],), (bout,), boTt[h])
        if dbg < 99:
            for h in range(4):
                dma(P, "sp", oT_d[h * 128:(h + 1) * 128, 0:512], oTt[h][:], (boTt[h],), (bout,), boTt[h])
        P.wait_all("sp", boTt)
        P.emit()
    return nc

def build_H(S_len=SEQ):
    nc = bass.Bass("TRN2", target_bir_lowering=False)
    es = ExitStack()
    hnT_d = nc.dram_tensor("hnT", [D, S_len], BF16, kind="ExternalInput").ap()
    w_d = nc.dram_tensor("w", [D, 1024], F32, kind="ExternalInput").ap()
    lbp_d = nc.dram_tensor("lbp", [128, 4], F32, kind="ExternalInput").ap()
    hng_d = nc.dram_tensor("hng", [128, 128], F32, kind="ExternalInput").ap()
    oT_d = nc.dram_tensor("oT", [256, S_len], BF16, kind="ExternalOutput").ap()
    NTILE = S_len // 512
    with es:
        P = Prog(nc, es)
        bin_ = P.buf("in")
        bout = P.buf("out")
        C = mixer_common(P)
        tri, b_tri = C.tri, C.b_tri
        W = P.sb("W", [128, 16, 1024], BF16); bW = P.buf("W")
        wv = w_d.rearrange("(k p) c -> p k c", p=128)
        for kq in range(4):
            dma(P, "pool", W[:, kq * 4:(kq + 1) * 4, :], wv[:, kq * 4:(kq + 1) * 4, :], (bin_,), (bW,), bW)
        lbp = P.sb("lbp", [128, 4], F32); blbp = P.buf("lbp")
        dma(P, "sp", lbp[:], lbp_d, (bin_,), (blbp,), blbp)
        hng = P.sb("hng", [128, 128], F32); bhng = P.buf("hng")
        dma(P, "sp", hng[:], hng_d, (bin_,), (bhng,), bhng)
        lb = P.sb("lb", [128, 4], F32); blb = P.buf("lb")
        TT(P, "dve", lb[:, 0:2], lbp[:, 2:4], lbp[:, 0:2], ALU.subtract, (blbp,), (blb,))
        ACT(P, lb[:, 2:4], lb[:, 0:2], AF.Sigmoid, (blb,), (blb,), scale=-1.0)
        ACT(P, lb[:, 0:2], lb[:, 0:2], AF.Sigmoid, (blb,), (blb,))
        Sf = [P.sb("Sf%d" % h, [128, 128], F32) for h in range(2)]
        bSf = [P.buf("Sf%d" % h) for h in range(2)]
        Sb = [[P.sb("Sb%d_%d" % (h, i), [128, 128], BF16) for i in range(2)] for h in range(2)]
        bSb = [[P.buf("Sb%d_%d" % (h, i)) for i in range(2)] for h in range(2)]
        for h in range(2):
            P.op("pool", lambda e, h=h: e.memset(Sf[h][:], 0.0), (), (bSf[h],))
            P.op("pool", lambda e, h=h: e.memset(Sb[h][0][:], 0.0), (), (bSb[h][0],))
        X = P.sb("X", [128, 16, 512], BF16); bX = P.buf("X")
        f32t = {nm: [P.sb("%s%d" % (nm, h), [128, 512], F32) for h in range(2)] for nm in ("qf", "fv", "kf", "Bt", "eB", "ek")}
        bf32 = {nm: [P.buf("%s%d" % (nm, h)) for h in range(2)] for nm in f32t}
        b16t = {nm: [P.sb("%s%d" % (nm, h), [128, 512], BF16) for h in range(2)] for nm in ("qd", "kh", "v16", "szh")}
        bb16 = {nm: [P.buf("%s%d" % (nm, h)) for h in range(2)] for nm in b16t}
        vtok = P.sb("vtokH", [64, 16, 128], BF16); bvtok = [P.buf("vtokH%d" % i) for i in range(16)]
        ktok = P.sb("ktokH", [64, 16, 128], BF16); bktok = [P.buf("ktokH%d" % i) for i in range(16)]
        oTt = [P.sb("oTt%d" % h, [128, 512], BF16) for h in range(2)]
        boTt = [P.buf("oTt%d" % h) for h in range(2)]
        PSb = [P.ps("PSb%d" % i, [128, 512], F32) for i in range(8)]
        bGA = [P.buf("GA0"), P.buf("GA1")]
        for b_ in bGA:
            b_.excl = True
        psA = PsumPool(P, "psA", [PSb[2], PSb[3]])
        psC = PsumPool(P, "psC", [PSb[4], PSb[5], PSb[6], PSb[7]])
        sc_p = Pool2(P, "sc", 4, [64, 64], BF16)
        f128 = Pool2(P, "f128", 8, [64, 128], F32)
        o_p = Pool2(P, "o", 4, [64, 128], F32)
        ssc_p = Pool2(P, "ssc", 2, [128, 128], F32)
        sm_p = Pool2(P, "sm", 8, [64, 2], F32)

        hv = hnT_d.rearrange("(k p) t -> p k t", p=128)
        for ti in range(NTILE):
            t0 = ti * 512
            dma(P, "sp", X[:], hv[:, :, t0:t0 + 512], (bin_,), (bX,), bX)
            for ct in range(8):
                ga = ct % 2
                hk = ct % 2
                for k in range(16):
                    mm(P, PSb[ga][:], W[:, k, ct * 128:(ct + 1) * 128], X[:, k, :], k == 0, k == 15, (bW, bX), (bGA[ga],))
                if ct < 2:
                    ACT(P, f32t["qf"][hk][:], PSb[ga][:], AF.Silu, (bGA[ga],), (bf32["qf"][hk],))
                elif ct < 4:
                    fv, bfv = f32t["fv"][hk], bf32["fv"][hk]
                    ACT(P, fv[:], PSb[ga][:], AF.Sigmoid, (bGA[ga],), (bfv,))
                    TS(P, "dve", fv[:], fv[:], lb[:, 2 + hk:3 + hk], lb[:, hk:hk + 1], ALU.mult, ALU.add, (bfv, blb), (bfv,))
                    TS(P, "dve", f32t["kf"][hk][:], fv[:], -1.0, 1.0, ALU.mult, ALU.add, (bfv,), (bf32["kf"][hk],))
                    ACT(P, fv[:], fv[:], AF.Ln, (bfv,), (bfv,))
                elif ct < 6:
                    CP(P, "act", b16t["v16"][hk][:], PSb[ga][:], (bGA[ga],), (bb16["v16"][hk],))
                else:
                    ACT(P, b16t["szh"][hk][:], PSb[ga][:], AF.Silu, (bGA[ga],), (bb16["szh"][hk],))
            for hk in range(2):
                Bt, bBt = f32t["Bt"][hk], bf32["Bt"][hk]
                for c in range(8):
                    P.op("dve", lambda e, hk=hk, c=c: e.tensor_tensor_scan(
                        f32t["Bt"][hk][:, c * 64:(c + 1) * 64], C.ones_f[:, 0:64], f32t["fv"][hk][:, c * 64:(c + 1) * 64],
                        0.0, ALU.mult, ALU.add), (bf32["fv"][hk], C.b_ones), (bBt,))
                ACT(P, f32t["eB"][hk][:], Bt[:], AF.Exp, (bBt,), (bf32["eB"][hk],))
                TT(P, "dve", b16t["qd"][hk][:], f32t["qf"][hk][:], f32t["eB"][hk][:], ALU.mult,
                   (bf32["qf"][hk], bf32["eB"][hk]), (bb16["qd"][hk],))
                TS(P, "dve", f32t["ek"][hk][:], Bt[:], -80.0, None, ALU.max, None, (bBt,), (bf32["ek"][hk],))
                ACT(P, f32t["ek"][hk][:], f32t["ek"][hk][:], AF.Exp, (bf32["ek"][hk],), (bf32["ek"][hk],), scale=-1.0)
                TT(P, "dve", b16t["kh"][hk][:], f32t["kf"][hk][:], f32t["ek"][hk][:], ALU.mult,
                   (bf32["kf"][hk], bf32["ek"][hk]), (bb16["kh"][hk],))
            jobs = []
            for c in range(8):
                for hk in range(2):
                    jobs.append((b16t["v16"][hk], bb16["v16"][hk], c, vtok[:, hk * 8 + c, :], bvtok[hk * 8 + c]))
                    jobs.append((b16t["kh"][hk], bb16["kh"][hk], c, ktok[:, hk * 8 + c, :], bktok[hk * 8 + c]))
            for gi, grp in enumerate(groups4(jobs)):
                psA.align()
                sl = []
                for (src, bsrc, c, dst, bdst) in grp:
                    pt, pbf = psA.get()
                    mm(P, pt[:64, :], src[:, c * 64:(c + 1) * 64], C.ident_b[:], True, True, (bsrc, C.b_identb), (pbf,))
                    sl.append((pt, pbf))
                for (src, bsrc, c, dst, bdst), (pt, pbf) in zip(grp, sl):
                    CP(P, "act" if gi % 2 == 0 else "dve", dst, pt[:64, :], (pbf,), (bdst,))
            psA.align()
            for c in range(8):
                par_i = (ti * 8 + c) % 2
                cs = slice(c * 64, (c + 1) * 64)
                R = {0: {}, 1: {}}
                psC.align()
                for hk in range(2):
                    p1, b1 = psC.get()
                    mm(P, p1[:64, 0:64], b16t["kh"][hk][:, cs], b16t["qd"][hk][:, cs], True, True,
                       (bb16["kh"][hk], bb16["qd"][hk]), (b1,))
                    p2, b2 = psC.get()
                    mm(P, p2[:64, :], b16t["qd"][hk][:, cs], Sb[hk][par_i][:], True, True,
                       (bb16["qd"][hk], bSb[hk][par_i]), (b2,))
                    R[hk].update(sc=p1, bsc=b1, qs=p2, bqs=b2)
                for hk in range(2):
                    scm, bscm = sc_p.get()
                    TT(P, "dve", scm, R[hk]["sc"][:64, 0:64], tri[:], ALU.mult, (R[hk]["bsc"], b_tri), (bscm,))
                    o, bo = o_p.get()
                    CP(P, "act", o, R[hk]["qs"][:64, :], (R[hk]["bqs"],), (bo,))
                    R[hk].update(scm=scm, bscm=bscm, o=o, bo=bo)
                psC.align()
                for hk in range(2):
                    p3, b3 = psC.get()
                    mm(P, p3[:64, :], R[hk]["scm"], vtok[:, hk * 8 + c, :], True, True, (R[hk]["bscm"], bvtok[hk * 8 + c]), (b3,))
                    p4, b4 = psC.get()
                    mm(P, p4[:, :], ktok[:, hk * 8 + c, :], vtok[:, hk * 8 + c, :], True, True,
                       (bktok[hk * 8 + c], bvtok[hk * 8 + c]), (b4,))
                    R[hk].update(sv=p3, bsv=b3, kv=p4, bkv=b4)
                for hk in range(2):
                    o, bo = R[hk]["o"], R[hk]["bo"]
                    TT(P, "dve", o, R[hk]["sv"][:64, :], o, ALU.add, (R[hk]["bsv"], bo), (bo,))
                    fl = f32t["eB"][hk][:, c * 64 + 63:c * 64 + 64]
                    ssc, bssc = ssc_p.get()
                    ACT(P, ssc, Sf[hk][:], AF.Copy, (bSf[hk], bf32["eB"][hk]), (bssc,), scale=fl)
                    STT(P, "dve", Sf[hk][:], R[hk]["kv"][:, :], fl, ssc, ALU.mult, ALU.add,
                        (R[hk]["bkv"], bf32["eB"][hk], bssc), (bSf[hk],))
                    CP(P, "act", Sb[hk][1 - par_i][:], Sf[hk][:], (bSf[hk],), (bSb[hk][1 - par_i],))
                emit_out_norm4(P, C, [(R[hk]["o"], R[hk]["bo"], b16t["szh"][hk][:, cs], bb16["szh"][hk],
                                       oTt[hk][:, cs], boTt[hk]) for hk in range(2)], hng[:64, :], bhng, f128, sm_p, psC)
            for h in range(2):
                dma(P, "sp", oT_d[h * 128:(h + 1) * 128, t0:t0 + 512], oTt[h][:], (boTt[h],), (bout,), boTt[h])
        P.wait_all("sp", boTt)
        P.emit()
    return nc


def _gvec(g):
    return np.ascontiguousarray(np.asarray(g, np.float32).reshape(16, 128).T)


def _host_G(c, w_in, conv, a_log, dt_bias, head_norm):
    qs = slice(256 * c, 256 * c + 256)
    ks = slice(2048 + 256 * c, 2048 + 256 * c + 256)
    vs = slice(4096 + 512 * c, 4096 + 512 * c + 512)
    zs = slice(8192 + 512 * c, 8192 + 512 * c + 512)
    bs = slice(12288 + 4 * c, 12288 + 4 * c + 4)
    as_ = slice(12320 + 4 * c, 12320 + 4 * c + 4)
    w = np.concatenate([w_in[:, qs], w_in[:, ks], w_in[:, vs], w_in[:, zs]], axis=1)
    wba = np.concatenate([w_in[:, bs], w_in[:, as_]], axis=1)
    cwc = np.concatenate([conv[:, qs], conv[:, ks], conv[:, vs]], axis=1)
    cw = cwc.reshape(4, 8, 128).transpose(2, 1, 0).reshape(128, 32)
    par = np.concatenate([np.tile(a_log[4 * c:4 * c + 4], 8), np.tile(dt_bias[4 * c:4 * c + 4], 8), head_norm])
    par = np.broadcast_to(par[None, :], (128, 192))
    return dict(w=np.ascontiguousarray(w, np.float32), wba=np.ascontiguousarray(wba, np.float32),
                cw=np.ascontiguousarray(cw, np.float32), par=np.ascontiguousarray(par, np.float32))


def _host_H(c, w_in, lbs, head_norm):
    sl = lambda base: slice(base + 256 * c, base + 256 * c + 256)
    w = np.concatenate([w_in[:, sl(0)], w_in[:, sl(2048)], w_in[:, sl(4096)], w_in[:, sl(6144)]], axis=1)
    l = lbs[:, 256 * c:256 * c + 256]
    lbp = np.stack([l[0, :128], l[0, 128:], l[1, :128], l[1, 128:]], axis=1)
    return dict(w=np.ascontiguousarray(w, np.float32), lbp=np.ascontiguousarray(lbp, np.float32),
                hng=np.ascontiguousarray(np.broadcast_to(head_norm, (128, 128)), np.float32))


_NC_CACHE = {}
_DBG = {}


def _nc(key, fn):
    if key not in _NC_CACHE:
        _NC_CACHE[key] = fn()
    return _NC_CACHE[key]


def kernel(x, gdn_norm, gdn_w_in, gdn_conv, gdn_a_log, gdn_dt_bias, gdn_head_norm, gdn_w_out,
           hgrn_norm, hgrn_w_in, hgrn_lower_bounds, hgrn_head_norm, hgrn_w_out,
           ffn_norm, ffn_w_gate, ffn_w_up, ffn_w_down, final_norm):
    f = lambda a: np.asarray(a, np.float32)
    x2 = np.ascontiguousarray(f(x).reshape(SEQ, D))
    cores = list(range(NCORE))
    tok = lambda c: slice(c * NT, (c + 1) * NT)
    ffn_norm, ffn_w_gate, ffn_w_up, ffn_w_down = f(ffn_norm), f(ffn_w_gate), f(ffn_w_up), f(ffn_w_down)
    res = run_bass_kernel_spmd(_nc("N", build_N), [{"h_in": x2[tok(c)], "g_next": _gvec(f(gdn_norm)[0])} for c in cores],
                               core_ids=cores)
    hnT = np.ascontiguousarray(np.concatenate([r["hnT_out"] for r in res.results], axis=1))
    _DBG["hn0T"] = hnT
    gw, gc = f(gdn_w_in)[0], f(gdn_conv)[0]
    maps = []
    for c in cores:
        m = _host_G(c, gw, gc, f(gdn_a_log)[0], f(gdn_dt_bias)[0], f(gdn_head_norm)[0])
        m["hnT"] = hnT
        maps.append(m)
    res = run_bass_kernel_spmd(_nc("G", build_G), maps, core_ids=cores)
    oT = np.concatenate([r["oT"] for r in res.results], axis=0)
    _DBG["o0T"] = oT
    maps = [{"h_in": x2[tok(c)], "oT": np.ascontiguousarray(oT[:, tok(c)]), "w_out": f(gdn_w_out)[0],
             "g_ffn": _gvec(ffn_norm[0]), "w_gate": ffn_w_gate[0], "w_up": ffn_w_up[0], "w_down": ffn_w_down[0],
             "g_next": _gvec(f(hgrn_norm)[0])} for c in cores]
    res = run_bass_kernel_spmd(_nc("T0", lambda: build_T(4096, False)), maps, core_ids=cores)
    h1 = [r["h_out"] for r in res.results]
    hnT = np.ascontiguousarray(np.concatenate([r["hnT_out"] for r in res.results], axis=1))
    _DBG["h1"] = h1
    _DBG["hn1T"] = hnT
    hw = f(hgrn_w_in)[0]
    maps = []
    for c in cores:
        m = _host_H(c, hw, f(hgrn_lower_bounds), f(hgrn_head_norm)[0])
        m["hnT"] = hnT
        maps.append(m)
    res = run_bass_kernel_spmd(_nc("H", build_H), maps, core_ids=cores)
    oT = np.concatenate([r["oT"] for r in res.results], axis=0)
    _DBG["o1T"] = oT
    gfb = np.ascontiguousarray(np.broadcast_to(f(final_norm)[None, :], (128, D)))
    maps = [{"h_in": h1[c], "oT": np.ascontiguousarray(oT[:, tok(c)]), "w_out": f(hgrn_w_out)[0],
             "g_ffn": _gvec(ffn_norm[1]), "w_gate": ffn_w_gate[1], "w_up": ffn_w_up[1], "w_down": ffn_w_down[1],
             "g_final_b": gfb} for c in cores]
    res = run_bass_kernel_spmd(_nc("T1", lambda: build_T(2048, True)), maps, core_ids=cores)
    out = np.concatenate([r["out"] for r in res.results], axis=0)
    return np.ascontiguousarray(out.reshape(1, SEQ, D).astype(np.float32))
```
